# Optimizing a Trainium2 kernel written in Bass

```python
import math
import jax, jax.numpy as jnp
from jax import lax
import numpy as np

D_MODEL = 2048
BATCH = 4
SEQ = 4096
DEPTH = 4

CONV_DIM = 512
CONV_WIDTH = 3
SWA_HEADS = 8
SWA_KV_HEADS = 2
SWA_GROUP = SWA_HEADS // SWA_KV_HEADS
SWA_HEAD_DIM = 64
SWA_WINDOW = 128
SWA_BLOCK = 128
MLA_HEADS = 4
MLA_Q_RANK = 512
MLA_KV_RANK = 256
MLA_NOPE_DIM = 128
MLA_ROPE_DIM = 64
MLA_V_DIM = 128
MLA_QK_DIM = MLA_NOPE_DIM + MLA_ROPE_DIM
MLA_BLOCK = 128
ROPE_THETA = 10000.0
POOL_WINDOWS = (2, 4, 8, 16)
POOL_GROUPS = 4
POOL_GROUP_DIM = 128
POOL_DIM = POOL_GROUPS * POOL_GROUP_DIM
N_BRANCHES = 4
BRANCH_DIM = 512
IN_SIZES = (CONV_DIM, CONV_DIM, CONV_DIM,
            SWA_HEADS * SWA_HEAD_DIM, SWA_KV_HEADS * SWA_HEAD_DIM, SWA_KV_HEADS * SWA_HEAD_DIM,
            MLA_Q_RANK, MLA_KV_RANK, MLA_ROPE_DIM,
            POOL_DIM,
            N_BRANCHES * D_MODEL)
IN_DIM = sum(IN_SIZES)
PEER_HEADS = 8
PEER_N_KEYS = 128
PEER_N_EXPERTS = PEER_N_KEYS * PEER_N_KEYS
PEER_KEY_DIM = 128
PEER_TOPK = 16
PEER_CHUNK = 128
DN_ALPHA = (2 * DEPTH) ** 0.25
DN_BETA = (8 * DEPTH) ** -0.25
NORM_EPS = 1e-5
NEG_INF = -1e30

kernel_name = "hybrid_gated_conv_swa_mla_pool_peer_deepnorm"


def _layer_norm(x, g, b):
    x32 = x.astype(jnp.float32)
    mu = jnp.mean(x32, axis=-1, keepdims=True)
    var = jnp.mean(jnp.square(x32 - mu), axis=-1, keepdims=True)
    return ((x32 - mu) * lax.rsqrt(var + NORM_EPS)).astype(x.dtype) * g + b


def _rms_norm(x, g):
    x32 = x.astype(jnp.float32)
    return (x32 * lax.rsqrt(jnp.mean(x32 * x32, axis=-1, keepdims=True) + NORM_EPS)).astype(x.dtype) * g


def _split_columns(h):
    offsets = np.cumsum(np.array(IN_SIZES))[:-1].tolist()
    return jnp.split(h, offsets, axis=-1)


def _alibi_slopes(n):
    return jnp.asarray(2.0 ** (-8.0 * np.arange(1, n + 1) / n), dtype=jnp.float32)


def _rope(x, cos, sin):
    x1, x2 = jnp.split(x, 2, axis=-1)
    c = cos[:, None, :]
    s = sin[:, None, :]
    return jnp.concatenate([x1 * c - x2 * s, x2 * c + x1 * s], axis=-1)


def _short_conv(u, gate_b, gate_c, conv_w):
    z = gate_c * u
    y = lax.conv_general_dilated(z, conv_w[:, None, :].astype(z.dtype), window_strides=(1,),
                                 padding=[(CONV_WIDTH - 1, 0)],
                                 dimension_numbers=('NWC', 'WIO', 'NWC'),
                                 feature_group_count=CONV_DIM)
    return gate_b * y


def _sliding_window_attention(q, k, v, sinks):
    b, s, _ = q.shape
    nb = s // SWA_BLOCK
    qb = q.reshape(b, nb, SWA_BLOCK, SWA_KV_HEADS, SWA_GROUP, SWA_HEAD_DIM)
    kb = k.reshape(b, nb, SWA_BLOCK, SWA_KV_HEADS, SWA_HEAD_DIM)
    vb = v.reshape(b, nb, SWA_BLOCK, SWA_KV_HEADS, SWA_HEAD_DIM)
    pad = ((0, 0), (1, 0), (0, 0), (0, 0), (0, 0))
    k_band = jnp.concatenate([jnp.pad(kb, pad)[:, :-1], kb], axis=2)
    v_band = jnp.concatenate([jnp.pad(vb, pad)[:, :-1], vb], axis=2)
    sc = jnp.einsum('bnqkgd,bnskd->bnkgqs', qb, k_band).astype(jnp.float32) * (SWA_HEAD_DIM ** -0.5)
    i = jnp.arange(SWA_BLOCK)[:, None]
    j = jnp.arange(2 * SWA_BLOCK)[None, :]
    dist = i - j + SWA_BLOCK
    key_pos = jnp.arange(nb)[:, None] * SWA_BLOCK - SWA_BLOCK + jnp.arange(2 * SWA_BLOCK)[None, :]
    mask = ((dist >= 0) & (dist < SWA_WINDOW))[None] & (key_pos >= 0)[:, None, :]
    slopes = _alibi_slopes(SWA_HEADS).reshape(SWA_KV_HEADS, SWA_GROUP)
    sc = sc - slopes[:, :, None, None] * dist.astype(jnp.float32)
    sc = jnp.where(mask[None, :, None, None], sc, NEG_INF)
    sink = jnp.broadcast_to(sinks.astype(jnp.float32).reshape(1, 1, SWA_KV_HEADS, SWA_GROUP, 1, 1),
                            sc.shape[:-1] + (1,))
    probs = jax.nn.softmax(jnp.concatenate([sc, sink], axis=-1), axis=-1)[..., :-1]
    o = jnp.einsum('bnkgqs,bnskd->bnqkgd', probs.astype(v.dtype), v_band)
    return o.reshape(b, s, SWA_HEADS * SWA_HEAD_DIM)


def _latent_attention(c_q, c_kv, k_rope_in, q_norm, w_q_up, kv_norm, w_kv_up, cos, sin):
    b, s, _ = c_q.shape
    q = (_rms_norm(c_q, q_norm) @ w_q_up).reshape(b, s, MLA_HEADS, MLA_QK_DIM)
    q = jnp.concatenate([q[..., :MLA_NOPE_DIM], _rope(q[..., MLA_NOPE_DIM:], cos, sin)], axis=-1)
    kv = (_rms_norm(c_kv, kv_norm) @ w_kv_up).reshape(b, s, MLA_HEADS, MLA_NOPE_DIM + MLA_V_DIM)
    k_pe = _rope(k_rope_in[:, :, None, :], cos, sin)
    k = jnp.concatenate([kv[..., :MLA_NOPE_DIM],
                         jnp.broadcast_to(k_pe, (b, s, MLA_HEADS, MLA_ROPE_DIM))], axis=-1)
    v = kv[..., MLA_NOPE_DIM:]
    nb = s // MLA_BLOCK
    qb = q.reshape(b, nb, MLA_BLOCK, MLA_HEADS, MLA_QK_DIM).transpose(1, 0, 2, 3, 4)
    key_idx = jnp.arange(s)

    def block(args):
        q_blk, n = args
        sc = jnp.einsum('bqhd,bshd->bhqs', q_blk, k).astype(jnp.float32) * (MLA_QK_DIM ** -0.5)
        t = n * MLA_BLOCK + jnp.arange(MLA_BLOCK)
        sc = jnp.where(key_idx[None, :] <= t[:, None], sc, NEG_INF)
        p = jax.nn.softmax(sc, axis=-1).astype(v.dtype)
        return jnp.einsum('bhqs,bshd->bqhd', p, v)

    o = lax.map(block, (qb, jnp.arange(nb)))
    return o.transpose(1, 0, 2, 3, 4).reshape(b, s, MLA_HEADS * MLA_V_DIM)


def _multiscale_pool(u, pool_w, pool_scale):
    b, s, _ = u.shape
    u32 = u.astype(jnp.float32)
    cs0 = jnp.pad(jnp.cumsum(u32, axis=1), ((0, 0), (1, 0), (0, 0)))
    cur = cs0[:, 1:]
    pos1 = jnp.arange(s) + 1
    groups = []
    for g, w in enumerate(POOL_WINDOWS):
        sl = slice(g * POOL_GROUP_DIM, (g + 1) * POOL_GROUP_DIM)
        prev = jnp.pad(cs0[:, :, sl], ((0, 0), (w - 1, 0), (0, 0)))[:, :s]
        count = jnp.minimum(pos1, w).astype(jnp.float32)[None, :, None]
        groups.append((cur[:, :, sl] - prev) / count - u32[:, :, sl])
    y = jnp.stack(groups, axis=2).astype(u.dtype)
    y = jnp.einsum('bsgc,gcd->bsgd', y, pool_w).reshape(b, s, POOL_DIM)
    return y * pool_scale


def _token_mixer(x, w_in, conv_w, swa_sinks, mla_q_norm, mla_w_q_up, mla_kv_norm, mla_w_kv_up,
                 pool_w, pool_scale, w_branch, w_out, cos, sin):
    b, s, d = x.shape
    (conv_u, conv_b, conv_c, swa_q, swa_k, swa_v, mla_cq, mla_ckv, mla_kr,
     pool_u, gate_pre) = _split_columns(x @ w_in)
    branches = (
        _short_conv(conv_u, conv_b, conv_c, conv_w),
        _sliding_window_attention(swa_q, swa_k, swa_v, swa_sinks),
        _latent_attention(mla_cq, mla_ckv, mla_kr, mla_q_norm, mla_w_q_up, mla_kv_norm, mla_w_kv_up, cos, sin),
        _multiscale_pool(pool_u, pool_w, pool_scale),
    )
    gates = jax.nn.sigmoid(gate_pre).reshape(b, s, N_BRANCHES, d)
    merged = gates[:, :, 0] * (branches[0] @ w_branch[0])
    for i in range(1, N_BRANCHES):
        merged = merged + gates[:, :, i] * (branches[i] @ w_branch[i])
    return merged @ w_out


def _peer(x, w_query, sub_keys, u_table, v_table):
    b, s, d = x.shape
    xt = x.reshape(-1, d)
    t = xt.shape[0]
    q = (xt @ w_query).reshape(t, PEER_HEADS, 2, PEER_KEY_DIM)
    sc = jnp.einsum('thpk,pnk->thpn', q, sub_keys).astype(jnp.float32)
    top_s, top_i = lax.top_k(sc, PEER_TOPK)
    cand = (top_s[:, :, 0, :, None] + top_s[:, :, 1, None, :]).reshape(t, PEER_HEADS, PEER_TOPK * PEER_TOPK)
    best_s, best_j = lax.top_k(cand, PEER_TOPK)
    i1 = jnp.take_along_axis(top_i[:, :, 0], best_j // PEER_TOPK, axis=-1)
    i2 = jnp.take_along_axis(top_i[:, :, 1], best_j % PEER_TOPK, axis=-1)
    expert = i1 * PEER_N_KEYS + i2
    g = jax.nn.softmax(best_s, axis=-1).astype(x.dtype)
    nc = t // PEER_CHUNK
    hk = PEER_HEADS * PEER_TOPK

    def chunk(args):
        xc, idx, gc = args
        u = jnp.take(u_table, idx, axis=0)
        a = jnp.einsum('cd,ckd->ck', xc, u)
        coef = gc * jax.nn.gelu(a, approximate=False)
        vv = jnp.take(v_table, idx, axis=0)
        return jnp.einsum('ck,ckd->cd', coef, vv)

    out = lax.map(chunk, (xt.reshape(nc, PEER_CHUNK, d), expert.reshape(nc, PEER_CHUNK, hk),
                          g.reshape(nc, PEER_CHUNK, hk)))
    return out.reshape(b, s, d)


def setup_inputs(seed: int = 0) -> dict:
    key = jax.random.key(seed)
    ks = jax.random.split(key, 20)
    L, D = DEPTH, D_MODEL

    def nrm(k, shape, scale):
        return jax.random.normal(k, shape, jnp.float32) * scale

    return {
        "x": nrm(ks[0], (BATCH, SEQ, D), 1.0),
        "w_in": nrm(ks[1], (L, D, IN_DIM), D ** -0.5),
        "conv_w": nrm(ks[2], (L, CONV_WIDTH, CONV_DIM), CONV_WIDTH ** -0.5),
        "swa_sinks": nrm(ks[3], (L, SWA_HEADS), 0.5),
        "mla_q_norm": 1.0 + nrm(ks[4], (L, MLA_Q_RANK), 0.02),
        "mla_w_q_up": nrm(ks[5], (L, MLA_Q_RANK, MLA_HEADS * MLA_QK_DIM), MLA_Q_RANK ** -0.5),
        "mla_kv_norm": 1.0 + nrm(ks[6], (L, MLA_KV_RANK), 0.02),
        "mla_w_kv_up": nrm(ks[7], (L, MLA_KV_RANK, MLA_HEADS * (MLA_NOPE_DIM + MLA_V_DIM)), MLA_KV_RANK ** -0.5),
        "pool_w": nrm(ks[8], (L, POOL_GROUPS, POOL_GROUP_DIM, POOL_GROUP_DIM), POOL_GROUP_DIM ** -0.5),
        "pool_scale": 1.0 + nrm(ks[9], (L, POOL_DIM), 0.02),
        "w_branch": nrm(ks[10], (L, N_BRANCHES, BRANCH_DIM, D), DN_BETA * BRANCH_DIM ** -0.5),
        "w_out": nrm(ks[11], (L, D, D), DN_BETA * D ** -0.5),
        "ln1_g": 1.0 + nrm(ks[12], (L, D), 0.02),
        "ln1_b": nrm(ks[13], (L, D), 0.02),
        "peer_w_query": nrm(ks[14], (L, D, PEER_HEADS * 2 * PEER_KEY_DIM), D ** -0.5),
        "peer_sub_keys": nrm(ks[15], (L, 2, PEER_N_KEYS, PEER_KEY_DIM), PEER_KEY_DIM ** -0.5),
        "peer_u": nrm(ks[16], (L, PEER_N_EXPERTS, D), D ** -0.5),
        "peer_v": nrm(ks[17], (L, PEER_N_EXPERTS, D), DN_BETA * (PEER_HEADS * PEER_TOPK) ** -0.5),
        "ln2_g": 1.0 + nrm(ks[18], (L, D), 0.02),
        "ln2_b": nrm(ks[19], (L, D), 0.02),
    }


def reference(x, w_in, conv_w, swa_sinks, mla_q_norm, mla_w_q_up, mla_kv_norm, mla_w_kv_up,
              pool_w, pool_scale, w_branch, w_out, ln1_g, ln1_b, peer_w_query, peer_sub_keys,
              peer_u, peer_v, ln2_g, ln2_b):
    s = x.shape[1]
    pos = jnp.arange(s, dtype=jnp.float32)
    inv_freq = ROPE_THETA ** (-jnp.arange(0, MLA_ROPE_DIM, 2, dtype=jnp.float32) / MLA_ROPE_DIM)
    ang = pos[:, None] * inv_freq[None, :]
    cos = jnp.cos(ang).astype(x.dtype)
    sin = jnp.sin(ang).astype(x.dtype)
    for l in range(DEPTH):
        mix = _token_mixer(x, w_in[l], conv_w[l], swa_sinks[l], mla_q_norm[l], mla_w_q_up[l],
                           mla_kv_norm[l], mla_w_kv_up[l], pool_w[l], pool_scale[l], w_branch[l],
                           w_out[l], cos, sin)
        h = _layer_norm(DN_ALPHA * x + mix, ln1_g[l], ln1_b[l])
        ffn = _peer(h, peer_w_query[l], peer_sub_keys[l], peer_u[l], peer_v[l])
        x = _layer_norm(DN_ALPHA * h + ffn, ln2_g[l], ln2_b[l])
    return x
```

```python
import numpy as np

import contextlib
import concourse.bass as bass
import concourse.mybir as mybir

F32 = mybir.dt.float32
BF16 = mybir.dt.bfloat16
U32 = mybir.dt.uint32
AF = mybir.ActivationFunctionType
ALU = mybir.AluOpType
AX = mybir.AxisListType


class Reg:
    __slots__ = ("name", "w", "r", "dsem", "dcnt")

    def __init__(self, name):
        self.name = name
        self.w = {}
        self.r = {}
        self.dsem = None
        self.dcnt = 0


def _merge(d, tok):
    if tok is None:
        return
    sem, val = tok
    k = id(sem)
    if k not in d or d[k][1] < val:
        d[k] = (sem, val)


class Eng:
    def __init__(self, fw, e, name, skip_self=False):
        self.fw = fw
        self.e = e
        self.name = name
        self.sem = fw.new_sem("s_" + name)
        self.cnt = 0
        self.waited = {}
        self.skip_self = skip_self

    def wait(self, tok):
        if tok is None:
            return
        sem, val = tok
        if self.skip_self and sem is self.sem:
            return
        k = id(sem)
        if self.waited.get(k, 0) >= val:
            return
        self.e.wait_ge(sem, val)
        self.waited[k] = val

    def wait_all(self, toks):
        for t in toks:
            self.wait(t)


class FW:
    def __init__(self, nc):
        self.nc = nc
        self.stack = contextlib.ExitStack()
        self.nsem = 0
        self.pe = Eng(self, nc.tensor, "pe", skip_self=True)
        self.act = Eng(self, nc.scalar, "act")
        self.dve = Eng(self, nc.vector, "dve")
        self.pool = Eng(self, nc.gpsimd, "pool")
        self.sp = Eng(self, nc.sync, "sp")
        self.engs = [self.pe, self.act, self.dve, self.pool, self.sp]
        self.all_dma_toks = {}

    def new_sem(self, name):
        self.nsem += 1
        return self.stack.enter_context(self.nc.semaphore(name))

    def sbuf(self, name, shape, dtype, stack=None):
        st = stack or self.stack
        return st.enter_context(self.nc.sbuf_tensor(name, list(shape), dtype))

    def psum(self, name, shape, dtype):
        return self.stack.enter_context(self.nc.psum_tensor(name, list(shape), dtype))

    def _deps(self, reads, writes):
        deps = {}
        for r in reads:
            for t in r.w.values():
                _merge(deps, t)
        for w in writes:
            for t in w.w.values():
                _merge(deps, t)
            for t in w.r.values():
                _merge(deps, t)
        return list(deps.values())

    def _commit(self, tok, reads, writes):
        for r in reads:
            _merge(r.r, tok)
        for w in writes:
            w.w = {}
            w.r = {}
            _merge(w.w, tok)

    def op(self, eng, fn, reads=(), writes=(), extra=()):
        eng.wait_all(self._deps(reads, writes))
        eng.wait_all(extra)
        inst = fn(eng.e)
        inst.then_inc(eng.sem, 1)
        eng.cnt += 1
        tok = (eng.sem, eng.cnt)
        self._commit(tok, reads, writes)
        return tok

    def mm(self, fns, reads=(), writes=()):
        eng = self.pe
        eng.wait_all(self._deps(reads, writes))
        inst = None
        for fn in fns:
            inst = fn(eng.e)
        inst.then_inc(eng.sem, 1)
        eng.cnt += 1
        tok = (eng.sem, eng.cnt)
        self._commit(tok, reads, writes)
        return tok

    def dma(self, eng, out, in_, semreg, reads=(), writes=(), **kw):
        eng.wait_all(self._deps(reads, writes))
        if semreg.dsem is None:
            semreg.dsem = {}
            semreg.dcnt = {}
        if eng.name not in semreg.dsem:
            semreg.dsem[eng.name] = self.new_sem("d_" + semreg.name + "_" + eng.name)
            semreg.dcnt[eng.name] = 0
        sem = semreg.dsem[eng.name]
        eng.e.dma_start(out=out, in_=in_, **kw).then_inc(sem, 16)
        semreg.dcnt[eng.name] += 16
        tok = (sem, semreg.dcnt[eng.name])
        self._commit(tok, reads, writes)
        _merge(self.all_dma_toks, tok)
        return tok

    def barrier(self, engs=None):
        toks = [(e.sem, e.cnt) for e in self.engs if e.cnt > 0]
        toks += list(self.all_dma_toks.values())
        for e in (engs or self.engs):
            e.wait_all(toks)


import math
import contextlib
import numpy as np

D_MODEL = 2048
TS = 512
NEG = -30000.0
ALPHA = 8 ** 0.25
MLA_SCALE = 192 ** -0.5

C_U, C_B, C_C, C_SQ, C_SK, C_SV, C_CQ, C_CKV, C_KA, C_KB, C_PU = 0, 4, 8, 12, 16, 18, 19, 23, 25, 26, 27
N_NG = 31


class Scope:
    uid = 0

    def __init__(self, B):
        self.B = B
        self.stack = contextlib.ExitStack()
        self.names = []

    def __enter__(self):
        return self

    def alloc(self, name, shape, dtype):
        Scope.uid += 1
        t = self.stack.enter_context(self.B.fw.nc.sbuf_tensor(f"{name}_{Scope.uid}", list(shape), dtype))
        setattr(self.B, name, t)
        self.names.append(name)
        return t

    def __exit__(self, *a):
        self.B.fw.barrier()
        self.stack.close()
        for n in self.names:
            setattr(self.B, n, None)
        return False


class LayerBufs:
    def __init__(self, fw):
        self.fw = fw
        self.regs = {}
        self.pst = [fw.psum(f"pb{i}", [128, 512], F32) for i in range(8)]
        self.PB = [Reg(f"pb{i}") for i in range(8)]
        self.rot = 0
        self.rots = {}
        sc = Scope(self)
        self.const_scope = sc
        sc.alloc("ident", [128, 128], BF16)
        sc.alloc("ones", [128, 128], BF16)
        sc.alloc("mlam", [128, 4, 512], BF16)
        sc.alloc("swab", [128, 2, 8, 128], F32)
        sc.alloc("swab0", [128, 8, 128], F32)
        sc.alloc("pbias", [128, 1], F32)
        sc.alloc("pcorr", [128, 4, 16], F32)
        sc.alloc("den", [128, 8], F32)
        sc.alloc("bst", [128, 4, 6], F32)
        sc.alloc("mv", [128, 2], F32)

    def R(self, name):
        if name not in self.regs:
            self.regs[name] = Reg(name)
        return self.regs[name]

    def bank(self):
        b = self.rot
        self.rot = (self.rot + 1) % 6
        return b

    def nxt(self, key, n):
        v = self.rots.get(key, 0)
        self.rots[key] = (v + 1) % n
        return v


def copy_evac(fw, which, out, in_, reads, writes):
    if which % 2 == 0:
        return fw.op(fw.act, lambda e: e.copy(out=out, in_=in_), reads=reads, writes=writes)
    return fw.op(fw.dve, lambda e: e.tensor_copy(out=out, in_=in_), reads=reads, writes=writes)


def load_consts(fw, B, T):
    q = fw.pool
    R = B.R("const")
    fw.dma(q, B.ident[:], T["ident"][:, :], R, writes=[R])
    fw.dma(q, B.ones[:], T["ones"][:, :], R, writes=[R])
    fw.dma(q, B.mlam[:].rearrange("p a b -> p (a b)"), T["mlam"][:, :], R, writes=[R])
    fw.dma(fw.sp, B.swab[:].rearrange("p a b c -> p (a b c)"), T["swab"][:, :], R, writes=[R])
    fw.dma(fw.sp, B.swab0[:].rearrange("p a b -> p (a b)"), T["swab0"][:, :], R, writes=[R])
    fw.dma(fw.sp, B.pbias[:], T["pbias"][:, :], R, writes=[R])
    fw.dma(fw.sp, B.pcorr[:].rearrange("p a b -> p (a b)"), T["pcorr"][:, :], R, writes=[R])


def alloc_mixer_state(B, sc, NK):
    sc.alloc("convw", [128, 4, 3], F32)
    sc.alloc("qn", [128, 4], F32)
    sc.alloc("kvn", [128, 2], F32)
    sc.alloc("pscale", [128, 4], F32)
    sc.alloc("esink", [128, 8], F32)
    sc.alloc("wqu", [128, 8, 4, 128], BF16)
    sc.alloc("wkn", [128, 4, 2, 128], BF16)
    sc.alloc("wkv", [128, 2, 512], BF16)
    sc.alloc("poolw", [128, 4, 128], BF16)
    sc.alloc("ckvnT", [128, 2, NK], BF16)
    sc.alloc("kpeT", [128, NK], BF16)
    sc.alloc("knh", [128, NK], BF16)
    sc.alloc("vh", [128, NK // 128, 129], BF16)
    sc.alloc("zbuf", [128, 4, 2 + TS], F32)
    sc.alloc("ubuf", [128, 4, 16 + TS], F32)
    sc.alloc("kTsw", [128, 2, 128 + TS], BF16)
    sc.alloc("vsw", [128, 5, 2, 65], BF16)
    sc.alloc("mrgT", [128, 16, TS], BF16)


def load_params(fw, B, W):
    q = fw.pool
    R = B.R("par")
    sp = fw.sp
    fw.dma(sp, B.convw[:].rearrange("p a b -> p (a b)"), W["convw"][:, :], R, writes=[R])
    fw.dma(sp, B.qn[:], W["qn"][:, :], R, writes=[R])
    fw.dma(sp, B.kvn[:], W["kvn"][:, :], R, writes=[R])
    fw.dma(sp, B.pscale[:], W["pscale"][:, :], R, writes=[R])
    fw.dma(sp, B.esink[:], W["sinks"][:, :], R, writes=[R])
    fw.dma(q, B.wqu[:].rearrange("p a b c -> p a (b c)"), W["wqu"].rearrange("a p c -> p a c"), R, writes=[R])
    fw.dma(q, B.wkn[:].rearrange("p a b c -> p a (b c)"), W["wkn"].rearrange("a p c -> p a c"), R, writes=[R])
    fw.dma(q, B.wkv[:].rearrange("p a b -> p (a b)"), W["wkv"][:, :], R, writes=[R])
    fw.dma(q, B.poolw[:], W["poolw"].rearrange("a p c -> p a c"), R, writes=[R])
    fw.op(fw.act, lambda e: e.activation(out=B.esink[:], in_=B.esink[:], func=AF.Exp), reads=[R], writes=[R])
    fw.op(fw.dve, lambda e: e.memset(B.vh[:, :, 128:129], 1.0), writes=[B.R("vh")])
    fw.op(fw.dve, lambda e: e.memset(B.vsw[:].rearrange("p a b c -> p (a b c)"), 0.0), writes=[B.R("vsw")])
    fw.op(fw.dve, lambda e: e.memset(B.vsw[:, :, :, 64:65], 1.0), writes=[B.R("vsw")])
    fw.op(fw.dve, lambda e: e.memset(B.zbuf[:].rearrange("p a b -> p (a b)"), 0.0), writes=[B.R("zbuf")])
    fw.op(fw.dve, lambda e: e.memset(B.ubuf[:].rearrange("p a b -> p (a b)"), 0.0), writes=[B.R("ubuf")])
    fw.op(fw.dve, lambda e: e.memset(B.kTsw[:].rearrange("p a b -> p (a b)"), 0.0), writes=[B.R("kTsw")])


def rmsnorm(fw, B, nch, gam, dst, R_dst, nfeat):
    src = B.cqf
    for c in range(nch):
        fw.op(fw.act, lambda e, c=c: e.activation(out=B.sq[:, c, :], in_=src[:, c, :], func=AF.Square),
              reads=[B.R("cqf")], writes=[B.R("sq")])
    b = B.bank()
    fw.mm([lambda e, c=c: e.matmul(B.pst[b][:, :], B.ones[:, :], B.sq[:, c, :], start=(c == 0), stop=(c == nch - 1)) for c in range(nch)],
          reads=[B.R("sq"), B.R("const")], writes=[B.PB[b]])
    fw.op(fw.act, lambda e: e.activation(out=B.rstd[:], in_=B.pst[b][:, :], func=AF.Sqrt, bias=1e-5, scale=1.0 / nfeat),
          reads=[B.PB[b]], writes=[B.R("rstd")])
    fw.op(fw.dve, lambda e: e.reciprocal(out=B.rstd[:], in_=B.rstd[:]), reads=[B.R("rstd")], writes=[B.R("rstd")])
    for c in range(nch):
        fw.op(fw.dve, lambda e, c=c: e.scalar_tensor_tensor(out=dst(c), in0=src[:, c, :], scalar=gam[:, c:c + 1], in1=B.rstd[:],
                                                           op0=ALU.mult, op1=ALU.mult),
              reads=[B.R("cqf"), B.R("rstd"), B.R("par")], writes=[R_dst])


def phase1a(fw, B, W, T, xsrc, tok0, tt, own, halo, first_own):
    pe, act, dve, pool, sp = fw.pe, fw.act, fw.dve, fw.pool, fw.sp
    pst, PB, R = B.pst, B.PB, B.R
    cols = slice(tt * TS, (tt + 1) * TS)
    with Scope(B) as sc:
        sc.alloc("xb", [128, 2048], BF16)
        sc.alloc("wng0", [128, 16, 128], BF16); sc.alloc("wng1", [128, 16, 128], BF16)
        sc.alloc("cs", [128, TS], F32); sc.alloc("sn", [128, TS], F32)
        sc.alloc("cu", [128, TS], F32); sc.alloc("cbb", [128, TS], F32)
        sc.alloc("cqf", [128, 4, TS], F32); sc.alloc("sq", [128, 4, TS], BF16)
        sc.alloc("rstd", [128, TS], F32)
        sc.alloc("t1", [128, TS], F32); sc.alloc("t2", [128, TS], F32)
        sc.alloc("ptmp", [128, 16 + TS], F32); sc.alloc("ptmp2", [128, 16 + TS], F32)
        sc.alloc("py", [128, TS], BF16)
        wngs = [B.wng0, B.wng1]
        xT = B.xT
        fw.dma(sp, B.cs[:], T["cs4"][:, cols], R("cs"), writes=[R("cs")])
        fw.dma(sp, B.sn[:], T["sn4"][:, cols], R("cs"), writes=[R("cs")])
        for s in range(4):
            fw.dma(pool, B.xb[:], xsrc[tok0 + s * 128:tok0 + (s + 1) * 128, :], R("xb"), reads=[R("xdst")], writes=[R("xb")])
            for q4 in range(4):
                b = B.bank()
                fw.mm([lambda e, dc=dc, b=b, i=i: e.matmul(pst[b][:, i * 128:(i + 1) * 128], B.xb[:, dc * 128:(dc + 1) * 128], B.ident[:, :],
                                                          start=True, stop=True) for i, dc in enumerate(range(4 * q4, 4 * q4 + 4))],
                      reads=[R("xb"), R("const")], writes=[PB[b]])
                copy_evac(fw, q4, xT[:, 4 * q4:4 * q4 + 4, s * 128:(s + 1) * 128], pst[b][:, :].rearrange("p (c t) -> p c t", t=128),
                          [PB[b]], [R("xT")])
        if own:
            fw.op(dve, lambda e: e.tensor_copy(out=B.zbuf[:, :, 0:2], in_=B.zbuf[:, :, TS:TS + 2]), reads=[R("zbuf")], writes=[R("zbuf")])
            fw.op(dve, lambda e: e.tensor_copy(out=B.ubuf[:, :, 0:16], in_=B.ubuf[:, :, TS:TS + 16]), reads=[R("ubuf")], writes=[R("ubuf")])
            fw.op(dve, lambda e: e.tensor_copy(out=B.kTsw[:, :, 0:128], in_=B.kTsw[:, :, TS:TS + 128]), reads=[R("kTsw")], writes=[R("kTsw")])
            fw.op(dve, lambda e: e.tensor_copy(out=B.vsw[:, 0, :, 0:64], in_=B.vsw[:, 4, :, 0:64]), reads=[R("vsw")], writes=[R("vsw")])
        conv_order = []
        for c in range(4):
            conv_order += [C_U + c, C_C + c] + ([C_B + c] if own else [])
        rest_own = [C_SQ, C_SQ + 1, C_SQ + 2, C_SQ + 3]
        common = [C_CKV, C_CKV + 1, C_KA, C_KB]
        if own:
            chunks = common + [C_CQ, C_CQ + 1, C_CQ + 2, C_CQ + 3] + rest_own + [C_SK, C_SK + 1, C_SV] + conv_order + [C_PU + g for g in range(4)]
        elif halo:
            chunks = common + [C_SK, C_SK + 1, C_SV] + conv_order + [C_PU + g for g in range(4)]
        else:
            chunks = common
        for ch in chunks:
            wi = B.nxt("wng", 2)
            wt = wngs[wi]; R_w = R(f"wng{wi}")
            fw.dma(pool, wt[:].rearrange("p a b -> p (a b)"), W["wng"][ch, :, :], R_w, writes=[R_w])
            b = B.bank()
            if ch == C_SV:
                for s in range(4):
                    fw.mm([lambda e, s=s, kc=kc: e.matmul(pst[b][:, s * 128:(s + 1) * 128], xT[:, kc, s * 128:(s + 1) * 128], wt[:, kc, :],
                                                          start=(kc == 0), stop=(kc == 15)) for kc in range(16)],
                          reads=[R("xT"), R_w], writes=[PB[b]])
                for s in range(4):
                    copy_evac(fw, 0, B.vsw[:, 1 + s, :, 0:64], pst[b][:, s * 128:(s + 1) * 128].rearrange("p (g d) -> p g d", d=64),
                              [PB[b]], [R("vsw")])
                continue
            fw.mm([lambda e, kc=kc: e.matmul(pst[b][:, :], wt[:, kc, :], xT[:, kc, :], start=(kc == 0), stop=(kc == 15)) for kc in range(16)],
                  reads=[R("xT"), R_w], writes=[PB[b]])
            ps = pst[b][:, :]
            if C_U <= ch < C_U + 4:
                fw.op(act, lambda e: e.copy(out=B.cu[:], in_=ps), reads=[PB[b]], writes=[R("cu")])
            elif C_C <= ch < C_C + 4:
                c = ch - C_C
                fw.op(dve, lambda e: e.tensor_tensor(out=B.zbuf[:, c, 2:2 + TS], in0=ps, in1=B.cu[:], op=ALU.mult),
                      reads=[PB[b], R("cu")], writes=[R("zbuf")])
            elif C_B <= ch < C_B + 4:
                c = ch - C_B
                fw.op(act, lambda e: e.copy(out=B.cbb[:], in_=ps), reads=[PB[b]], writes=[R("cbb")])
                fw.op(dve, lambda e: e.tensor_scalar(out=B.t1[:], in0=B.zbuf[:, c, 0:TS], scalar1=B.convw[:, c, 0:1], scalar2=None, op0=ALU.mult),
                      reads=[R("zbuf"), R("par")], writes=[R("t1")])
                fw.op(dve, lambda e: e.scalar_tensor_tensor(out=B.t1[:], in0=B.zbuf[:, c, 1:1 + TS], scalar=B.convw[:, c, 1:2], in1=B.t1[:],
                                                           op0=ALU.mult, op1=ALU.add), reads=[R("zbuf"), R("par"), R("t1")], writes=[R("t1")])
                fw.op(dve, lambda e: e.scalar_tensor_tensor(out=B.t1[:], in0=B.zbuf[:, c, 2:2 + TS], scalar=B.convw[:, c, 2:3], in1=B.t1[:],
                                                           op0=ALU.mult, op1=ALU.add), reads=[R("zbuf"), R("par"), R("t1")], writes=[R("t1")])
                fw.op(dve, lambda e: e.tensor_tensor(out=B.brT[:, 0, c, :], in0=B.t1[:], in1=B.cbb[:], op=ALU.mult),
                      reads=[R("t1"), R("cbb")], writes=[R("br0")])
            elif C_SQ <= ch < C_SQ + 4:
                c = ch - C_SQ
                fw.op(act, lambda e: e.copy(out=B.qsw[0:64, 2 * c, :], in_=pst[b][0:64, :]), reads=[PB[b]], writes=[R("qsw")])
                fw.op(act, lambda e: e.copy(out=B.qsw[64:128, 2 * c + 1, :], in_=pst[b][64:128, :]), reads=[PB[b]], writes=[R("qsw")])
            elif C_SK <= ch < C_SK + 2:
                copy_evac(fw, ch, B.kTsw[:, ch - C_SK, 128:128 + TS], ps, [PB[b]], [R("kTsw")])
            elif C_CQ <= ch < C_CQ + 4:
                c = ch - C_CQ
                fw.op(act, lambda e: e.copy(out=B.cqf[:, c, :], in_=ps), reads=[PB[b]], writes=[R("cqf")])
                if c == 3:
                    rmsnorm(fw, B, 4, B.qn, lambda c: B.cqn[:, c, :], R("cqn"), 512)
            elif C_CKV <= ch < C_CKV + 2:
                c = ch - C_CKV
                fw.op(act, lambda e: e.copy(out=B.cqf[:, c, :], in_=ps), reads=[PB[b]], writes=[R("cqf")])
                if c == 1:
                    rmsnorm(fw, B, 2, B.kvn, lambda c: B.ckvnT[:, c, cols], R("ckvnT"), 256)
            elif ch == C_KA:
                fw.op(dve, lambda e: e.tensor_tensor(out=B.t1[:], in0=ps, in1=B.cs[:], op=ALU.mult), reads=[PB[b], R("cs")], writes=[R("t1")])
            elif ch == C_KB:
                fw.op(dve, lambda e: e.tensor_tensor(out=B.t2[:], in0=ps, in1=B.sn[:], op=ALU.mult), reads=[PB[b], R("cs")], writes=[R("t2")])
                fw.op(dve, lambda e: e.tensor_tensor(out=B.kpeT[:, cols], in0=B.t1[:], in1=B.t2[:], op=ALU.add),
                      reads=[R("t1"), R("t2")], writes=[R("kpeT")])
            elif C_PU <= ch < C_PU + 4:
                g = ch - C_PU
                fw.op(act, lambda e: e.copy(out=B.ubuf[:, g, 16:16 + TS], in_=ps), reads=[PB[b]], writes=[R("ubuf")])
                if not own:
                    continue
                L = 16 + TS
                src = B.ubuf[:, g, :]
                bufs = [(B.ptmp, R("ptmp")), (B.ptmp2, R("ptmp2"))]
                cur = None
                for lev in range(g + 1):
                    sh = 1 << lev
                    dstb, R_d = bufs[lev % 2]
                    if cur is None:
                        fw.op(dve, lambda e: e.tensor_tensor(out=dstb[:, sh:L], in0=src[:, sh:L], in1=src[:, 0:L - sh], op=ALU.add),
                              reads=[R("ubuf")], writes=[R_d])
                    else:
                        cb_, R_c = cur
                        fw.op(dve, lambda e: e.tensor_tensor(out=dstb[:, 2 * sh - 1:L], in0=cb_[:, 2 * sh - 1:L], in1=cb_[:, sh - 1:L - sh], op=ALU.add),
                              reads=[R_c], writes=[R_d])
                    cur = (dstb, R_d)
                cb_, R_c = cur
                wd = 2 << g
                if first_own:
                    fw.op(dve, lambda e: e.tensor_tensor(out=cb_[:, 16:32], in0=cb_[:, 16:32], in1=B.pcorr[:, g, :], op=ALU.mult),
                          reads=[R_c, R("const")], writes=[R_c])
                fw.op(dve, lambda e: e.scalar_tensor_tensor(out=B.py[:], in0=cb_[:, 16:L], scalar=1.0 / wd, in1=B.ubuf[:, g, 16:L],
                                                           op0=ALU.mult, op1=ALU.subtract), reads=[R_c, R("ubuf")], writes=[R("py")])
                b2 = B.bank()
                fw.mm([lambda e: e.matmul(pst[b2][:, :], B.poolw[:, g, :], B.py[:], start=True, stop=True)],
                      reads=[R("py"), R("par")], writes=[PB[b2]])
                fw.op(dve, lambda e: e.tensor_scalar(out=B.brT[:, 3, g, :], in0=pst[b2][:, :], scalar1=B.pscale[:, g:g + 1], scalar2=None, op0=ALU.mult),
                      reads=[PB[b2], R("par")], writes=[R("br3")])
        if own:
            for h in range(4):
                b = B.bank()
                fw.mm([lambda e, kc=kc: e.matmul(pst[b][:, :], B.wqu[:, h, kc, :], B.cqn[:, kc, :], start=(kc == 0), stop=(kc == 3)) for kc in range(4)],
                      reads=[R("cqn"), R("par")], writes=[PB[b]])
                copy_evac(fw, h, B.qnT[:, h, :], pst[b][:, :], [PB[b]], [R("qnT")])
            for pr in range(2):
                ba = B.bank()
                fw.mm([lambda e, kc=kc: e.matmul(pst[ba][:, :], B.wqu[:, 4 + 2 * pr, kc, :], B.cqn[:, kc, :], start=(kc == 0), stop=(kc == 3)) for kc in range(4)],
                      reads=[R("cqn"), R("par")], writes=[PB[ba]])
                fw.op(dve, lambda e: e.tensor_tensor(out=B.t1[:], in0=pst[ba][:, :], in1=B.cs[:], op=ALU.mult), reads=[PB[ba], R("cs")], writes=[R("t1")])
                bb = B.bank()
                fw.mm([lambda e, kc=kc: e.matmul(pst[bb][:, :], B.wqu[:, 5 + 2 * pr, kc, :], B.cqn[:, kc, :], start=(kc == 0), stop=(kc == 3)) for kc in range(4)],
                      reads=[R("cqn"), R("par")], writes=[PB[bb]])
                fw.op(dve, lambda e: e.tensor_tensor(out=B.t2[:], in0=pst[bb][:, :], in1=B.sn[:], op=ALU.mult), reads=[PB[bb], R("cs")], writes=[R("t2")])
                fw.op(dve, lambda e: e.tensor_tensor(out=B.qpe[0:64, 2 * pr, :], in0=B.t1[0:64, :], in1=B.t2[0:64, :], op=ALU.add),
                      reads=[R("t1"), R("t2")], writes=[R("qpe")])
                fw.op(dve, lambda e: e.tensor_tensor(out=B.qpe[64:128, 2 * pr + 1, :], in0=B.t1[64:128, :], in1=B.t2[64:128, :], op=ALU.add),
                      reads=[R("t1"), R("t2")], writes=[R("qpe")])


def phase1b(fw, B, T, tt, first_own, nprev_chunks):
    pe, act, dve, pool, sp = fw.pe, fw.act, fw.dve, fw.pool, fw.sp
    pst, PB, R = B.pst, B.PB, B.R
    kc0 = tt * 4
    with Scope(B) as sc:
        sc.alloc("scb0", [128, 512], F32); sc.alloc("scb1", [128, 512], F32)
        sc.alloc("eb0", [128, 512], BF16); sc.alloc("eb1", [128, 512], BF16)
        sc.alloc("osw", [128, 512], BF16)
        sc.alloc("omla", [128, 4, 512], BF16)
        scbs = [B.scb0, B.scb1]; ebs = [B.eb0, B.eb1]
        oa = [6, 7]
        import os
        SKIP = os.environ.get("SKIP", "")
        SP = int(os.environ.get("SWAPART", "9"))
        for j in range(4):
            if "swa" in SKIP:
                break
            for g in range(2):
                for c in range(2):
                    b = B.bank()
                    fw.mm([lambda e, hl=hl: e.matmul(
                        pst[b][:, hl * 128:(hl + 1) * 128],
                        B.kTsw[:, g, (j + c) * 128:(j + c + 1) * 128],
                        B.qsw[:, 4 * g + hl, j * 128:(j + 1) * 128],
                        start=True, stop=True) for hl in range(4)],
                        reads=[R("kTsw"), R("qsw")], writes=[PB[b]])
                    ei = B.nxt("eb", 2)
                    if first_own and j == 0 and c == 0:
                        bias = B.swab0[:, 4 * g:4 * g + 4, :]
                    else:
                        bias = B.swab[:, c, 4 * g:4 * g + 4, :]
                    if SP < 2:
                        fw.op(dve, lambda e: e.tensor_copy(out=scbs[ei][:], in_=pst[b][:, :]), reads=[PB[b]], writes=[R(f"scb{ei}")])
                        continue
                    fw.op(dve, lambda e: e.scalar_tensor_tensor(
                        out=scbs[ei][:].rearrange("p (h q) -> p h q", q=128), in0=pst[b][:, :].rearrange("p (h q) -> p h q", q=128),
                        scalar=0.125, in1=bias, op0=ALU.mult, op1=ALU.add), reads=[PB[b], R("const")], writes=[R(f"scb{ei}")])
                    if SP < 3:
                        continue
                    fw.op(act, lambda e: e.activation(out=ebs[ei][:], in_=scbs[ei][:], func=AF.Exp), reads=[R(f"scb{ei}")], writes=[R(f"eb{ei}")])
                    if SP < 4:
                        continue
                    ob = oa[g]
                    fw.mm([lambda e, hl=hl: e.matmul(
                        pst[ob][:, hl * 65:(hl + 1) * 65], ebs[ei][:, hl * 128:(hl + 1) * 128], B.vsw[:, j + c, g, :],
                        start=(c == 0 and hl == 0), stop=(c == 1), skip_group_check=True) for hl in range(4)],
                        reads=[R(f"eb{ei}"), R("vsw")], writes=[PB[ob]])
            if SP < 5:
                continue
            for g in range(2):
                ob = oa[g]
                ov = pst[ob][:, 0:260].rearrange("p (h d) -> p h d", d=65)
                fw.op(dve, lambda e: e.tensor_tensor(out=B.den[:, 4 * g:4 * g + 4], in0=ov[:, :, 64], in1=B.esink[:, 4 * g:4 * g + 4], op=ALU.add),
                      reads=[PB[ob], R("par")], writes=[R("den")])
                fw.op(dve, lambda e: e.reciprocal(out=B.den[:, 4 * g:4 * g + 4], in_=B.den[:, 4 * g:4 * g + 4]), reads=[R("den")], writes=[R("den")])
                fw.op(dve, lambda e: e.tensor_tensor(
                    out=B.osw[:, 256 * g:256 * g + 256].rearrange("p (h d) -> p h d", d=64), in0=ov[:, :, 0:64],
                    in1=B.den[:, 4 * g:4 * g + 4].unsqueeze(2).to_broadcast([128, 4, 64]), op=ALU.mult),
                    reads=[PB[ob], R("den")], writes=[R("osw")])
            if SP < 6:
                continue
            b = B.bank()
            fw.mm([lambda e, c=c: e.matmul(pst[b][:, c * 128:(c + 1) * 128], B.osw[:, c * 128:(c + 1) * 128], B.ident[:, :], start=True, stop=True)
                   for c in range(4)], reads=[R("osw"), R("const")], writes=[PB[b]])
            copy_evac(fw, j, B.brT[:, 1, :, j * 128:(j + 1) * 128], pst[b][:, :].rearrange("p (c q) -> p c q", q=128), [PB[b]], [R("br1")])

        nkc = kc0 + 4
        for h in range(4):
            if "mla" in SKIP:
                break
            ph = 64 * (h % 2)
            for t in range(tt + 1):
                b = B.bank()
                fw.mm([lambda e, kc=kc: e.matmul(pst[b][:, :], B.wkn[:, h, kc, :], B.ckvnT[:, kc, t * TS:(t + 1) * TS], start=(kc == 0), stop=(kc == 1))
                       for kc in range(2)], reads=[R("ckvnT"), R("par")], writes=[PB[b]])
                copy_evac(fw, t, B.knh[:, t * TS:(t + 1) * TS], pst[b][:, :], [PB[b]], [R("knh")])
            for t in range(tt + 1):
                b = B.bank()
                for s in range(4):
                    kcg = 4 * t + s
                    fw.mm([lambda e, kc=kc: e.matmul(pst[b][:, s * 128:(s + 1) * 128], B.ckvnT[:, kc, kcg * 128:(kcg + 1) * 128],
                                                     B.wkv[:, kc, h * 128:(h + 1) * 128], start=(kc == 0), stop=(kc == 1)) for kc in range(2)],
                          reads=[R("ckvnT"), R("par")], writes=[PB[b]])
                copy_evac(fw, t + 1, B.vh[:, 4 * t:4 * t + 4, 0:128], pst[b][:, :].rearrange("p (s d) -> p s d", d=128), [PB[b]], [R("vh")])
            for kcg in range(nkc):
                b = B.bank()
                kcs = slice(kcg * 128, (kcg + 1) * 128)
                j = kcg - kc0
                fns = [lambda e: e.matmul(pst[b][:, :], B.knh[:, kcs], B.qnT[:, h, :], start=True, stop=False),
                       lambda e: e.matmul(pst[b][:, :], B.kpeT[:, kcs], B.qpe[:, h, :], start=False, stop=(j < 0))]
                if j >= 0:
                    fns.append(lambda e: e.matmul(pst[b][:, :], B.ident[:, :], B.mlam[:, j, :], start=False, stop=True))
                fw.mm(fns, reads=[R("knh"), R("kpeT"), R("qnT"), R("qpe"), R("const")], writes=[PB[b]])
                ei = B.nxt("eb", 2)
                isprev = kcg < nprev_chunks
                fw.op(act, lambda e: e.activation(out=ebs[ei][:], in_=pst[b][:, :], func=AF.Exp, scale=MLA_SCALE,
                                                  bias=(B.pbias[:, 0:1] if isprev else 0.0)),
                      reads=[PB[b], R("const")], writes=[R(f"eb{ei}")])
                for s in range(4):
                    if j > s:
                        continue
                    ob = oa[s // 2]
                    fw.mm([lambda e: e.matmul(pst[ob][:, (s % 2) * 129:(s % 2) * 129 + 129], ebs[ei][:, s * 128:(s + 1) * 128],
                                              B.vh[:, kcg, :], start=(kcg == 0 and s % 2 == 0), stop=(kcg == kc0 + s), skip_group_check=True)],
                          reads=[R(f"eb{ei}"), R("vh")], writes=[PB[ob]])
            for s in range(4):
                ob = oa[s // 2]
                o0 = (s % 2) * 129
                fw.op(dve, lambda e: e.reciprocal(out=B.den[:, s:s + 1], in_=pst[ob][:, o0 + 128:o0 + 129]), reads=[PB[ob]], writes=[R("den")])
                fw.op(dve, lambda e: e.tensor_scalar(out=B.omla[:, s, h * 128:(h + 1) * 128], in0=pst[ob][:, o0:o0 + 128],
                                                     scalar1=B.den[:, s:s + 1], scalar2=None, op0=ALU.mult),
                      reads=[PB[ob], R("den")], writes=[R("omla")])
        for s in range(4):
            b = B.bank()
            fw.mm([lambda e, c=c: e.matmul(pst[b][:, c * 128:(c + 1) * 128], B.omla[:, s, c * 128:(c + 1) * 128], B.ident[:, :], start=True, stop=True)
                   for c in range(4)], reads=[R("omla"), R("const")], writes=[PB[b]])
            copy_evac(fw, s, B.brT[:, 2, :, s * 128:(s + 1) * 128], pst[b][:, :].rearrange("p (c q) -> p c q", q=128), [PB[b]], [R("br2")])


def phase2(fw, B, W):
    pe, act, dve, pool, sp = fw.pe, fw.act, fw.dve, fw.pool, fw.sp
    pst, PB, R = B.pst, B.PB, B.R
    xT = B.xT
    with Scope(B) as sc:
        for i in range(3):
            sc.alloc(f"wgb{i}", [128, 2560], BF16)
        sc.alloc("sig0", [128, TS], F32); sc.alloc("sig1", [128, TS], F32)
        sc.alloc("macc", [128, TS], F32); sc.alloc("mtmp", [128, TS], F32)
        wgbs = [B.wgb0, B.wgb1, B.wgb2]; sigs = [B.sig0, B.sig1]
        for dcn in range(16):
            for i in range(4):
                gi = B.nxt("wgb", 3)
                wt = wgbs[gi]; R_w = R(f"wgb{gi}")
                fw.dma(pool, wt[:], W["wgb"][i * 16 + dcn, :, :], R_w, writes=[R_w])
                bg = B.bank()
                fw.mm([lambda e, kc=kc: e.matmul(pst[bg][:, :], wt[:, kc * 128:(kc + 1) * 128], xT[:, kc, :], start=(kc == 0), stop=(kc == 15))
                       for kc in range(16)], reads=[R("xT"), R_w], writes=[PB[bg]])
                bp = B.bank()
                fw.mm([lambda e, cc=cc: e.matmul(pst[bp][:, :], wt[:, 2048 + cc * 128:2048 + (cc + 1) * 128], B.brT[:, i, cc, :],
                                                 start=(cc == 0), stop=(cc == 3)) for cc in range(4)],
                      reads=[R(f"br{i}"), R_w], writes=[PB[bp]])
                si = B.nxt("sig", 2)
                fw.op(act, lambda e: e.activation(out=sigs[si][:], in_=pst[bg][:, :], func=AF.Sigmoid), reads=[PB[bg]], writes=[R(f"sig{si}")])
                if i == 0:
                    fw.op(dve, lambda e: e.tensor_tensor(out=B.macc[:], in0=pst[bp][:, :], in1=sigs[si][:], op=ALU.mult),
                          reads=[PB[bp], R(f"sig{si}")], writes=[R("macc")])
                else:
                    fw.op(dve, lambda e: e.tensor_tensor(out=B.mtmp[:], in0=pst[bp][:, :], in1=sigs[si][:], op=ALU.mult),
                          reads=[PB[bp], R(f"sig{si}")], writes=[R("mtmp")])
                    if i < 3:
                        fw.op(dve, lambda e: e.tensor_tensor(out=B.macc[:], in0=B.macc[:], in1=B.mtmp[:], op=ALU.add),
                              reads=[R("macc"), R("mtmp")], writes=[R("macc")])
                    else:
                        fw.op(dve, lambda e: e.tensor_tensor(out=B.mrgT[:, dcn, :], in0=B.macc[:], in1=B.mtmp[:], op=ALU.add),
                              reads=[R("macc"), R("mtmp")], writes=[R("mrgT")])


def layer_norm(fw, B, y, R_y, gam, bet, R_gb):
    dve, pool, act = fw.dve, fw.pool, fw.act
    R = B.R
    for c in range(4):
        fw.op(dve, lambda e, c=c: e.bn_stats(out=B.bst[:, c, :], in_=y[:, c * 512:(c + 1) * 512]), reads=[R_y], writes=[R("bst")])
    fw.op(dve, lambda e: e.bn_aggr(out=B.mv[:], in_=B.bst[:]), reads=[R("bst")], writes=[R("mv")])
    fw.op(act, lambda e: e.activation(out=B.mv[:, 1:2], in_=B.mv[:, 1:2], func=AF.Sqrt, bias=1e-5, scale=1.0), reads=[R("mv")], writes=[R("mv")])
    fw.op(dve, lambda e: e.reciprocal(out=B.mv[:, 1:2], in_=B.mv[:, 1:2]), reads=[R("mv")], writes=[R("mv")])
    fw.op(dve, lambda e: e.tensor_scalar(out=y, in0=y, scalar1=B.mv[:, 0:1], scalar2=B.mv[:, 1:2], op0=ALU.subtract, op1=ALU.mult),
          reads=[R_y, R("mv")], writes=[R_y])
    fw.op(dve, lambda e: e.tensor_tensor(out=y, in0=y, in1=gam[:], op=ALU.mult), reads=[R_y, R_gb], writes=[R_y])
    fw.op(dve, lambda e: e.tensor_tensor(out=y, in0=y, in1=bet[:], op=ALU.add), reads=[R_y, R_gb], writes=[R_y])


def phase3(fw, B, W, xsrc, tok0, h_dst, hT_dst=None):
    pe, act, dve, pool, sp = fw.pe, fw.act, fw.dve, fw.pool, fw.sp
    pst, PB, R = B.pst, B.PB, B.R
    with Scope(B) as sc:
        sc.alloc("wo0", [128, 16, 512], BF16); sc.alloc("wo1", [128, 16, 512], BF16)
        sc.alloc("xres", [128, 4, 512], F32)
        sc.alloc("yv", [128, 4, 2048], F32)
        sc.alloc("lng", [128, 2048], F32); sc.alloc("lnb", [128, 2048], F32)
        wos = [B.wo0, B.wo1]
        fw.dma(sp, B.lng[:], W["ln1g"][:, :], R("lngb"), writes=[R("lngb")])
        fw.dma(sp, B.lnb[:], W["ln1b"][:, :], R("lngb"), writes=[R("lngb")])
        for nb in range(4):
            wt = wos[nb % 2]; R_w = R(f"wo{nb % 2}")
            fw.dma(pool, wt[:].rearrange("p a b -> p (a b)"), W["wout"][nb, :, :], R_w, writes=[R_w])
            fw.dma(sp, B.xres[:], xsrc[tok0:tok0 + TS, nb * 512:(nb + 1) * 512].rearrange("(s p) d -> p s d", p=128), R("xres"), reads=[R("xdst")], writes=[R("xres")])
            for s in range(4):
                b = B.bank()
                fw.mm([lambda e, kc=kc: e.matmul(pst[b][:, :], B.mrgT[:, kc, s * 128:(s + 1) * 128], wt[:, kc, :], start=(kc == 0), stop=(kc == 15))
                       for kc in range(16)], reads=[R("mrgT"), R_w], writes=[PB[b]])
                fw.op(dve, lambda e: e.scalar_tensor_tensor(out=B.yv[:, s, nb * 512:(nb + 1) * 512], in0=B.xres[:, s, :],
                                                           scalar=ALPHA, in1=pst[b][:, :], op0=ALU.mult, op1=ALU.add),
                      reads=[PB[b], R("xres")], writes=[R(f"yv{s}")])
        for s in range(4):
            layer_norm(fw, B, B.yv[:, s, :], R(f"yv{s}"), B.lng, B.lnb, R("lngb"))
            fw.dma(sp, h_dst[tok0 + s * 128:tok0 + (s + 1) * 128, :], B.yv[:, s, :], R(f"yv{s}"), reads=[R(f"yv{s}")], writes=[R("hdst")])


def emit_mixer(fw, B, W, T, x_own, x_prev, h_dst, nprev_tiles, nown_tiles, dbg=None):
    NK = (nprev_tiles + nown_tiles) * TS
    with Scope(B) as st:
        alloc_mixer_state(B, st, NK)
        load_params(fw, B, W)
        import os
        STOP = os.environ.get("STOP", "")
        for tt in range(nprev_tiles + nown_tiles):
            if STOP.startswith("tile") and tt >= int(STOP[4:]):
                break
            own = tt >= nprev_tiles
            halo = (tt == nprev_tiles - 1)
            first_own = (tt == nprev_tiles)
            xsrc = x_own if own else x_prev
            tok0 = (tt - nprev_tiles) * TS if own else tt * TS
            with Scope(B) as s12:
                s12.alloc("xT", [128, 16, TS], BF16)
                if own:
                    s12.alloc("brT", [128, 4, 4, TS], BF16)
                    s12.alloc("qsw", [128, 8, TS], BF16)
                    s12.alloc("cqn", [128, 4, TS], BF16)
                    s12.alloc("qnT", [128, 4, TS], BF16)
                    s12.alloc("qpe", [128, 4, TS], BF16)
                    fw.op(fw.dve, lambda e: e.memset(B.qsw[:].rearrange("p a b -> p (a b)"), 0.0), writes=[B.R("qsw")])
                    fw.op(fw.dve, lambda e: e.memset(B.qpe[:].rearrange("p a b -> p (a b)"), 0.0), writes=[B.R("qpe")])
                else:
                    s12.alloc("cqn", [128, 4, TS], BF16)
                phase1a(fw, B, W, T, xsrc, tok0, tt, own, halo, first_own)
                if own and STOP == "p1a":
                    break
                if own:
                    phase1b(fw, B, T, tt, first_own, nprev_tiles * 4)
                    if STOP == "p1b":
                        break
                    if dbg is not None:
                        for i in range(4):
                            fw.dma(fw.pool, dbg["brT"][i].rearrange("(c p) t -> p c t", p=128)[:, :, tok0:tok0 + TS], B.brT[:, i, :, :],
                                   B.R(f"br{i}"), reads=[B.R(f"br{i}")])
                    phase2(fw, B, W)
                    if STOP == "p2":
                        break
            if own:
                phase3(fw, B, W, xsrc, tok0, h_dst)
                if STOP == "p3":
                    break


def peer_consts(fw, B, T):
    sc = B.const_scope
    sc.alloc("identf", [128, 128], F32)
    sc.alloc("iotab", [128, 128], BF16)
    sc.alloc("iota16", [128, 16], F32)
    R = B.R("const")
    fw.dma(fw.sp, B.identf[:], T["ident"][:, :], R, writes=[R])
    fw.dma(fw.pool, B.iotab[:], T["iota"][:, :], R, writes=[R])
    fw.dma(fw.sp, B.iota16[:], T["iota16"][:, :], R, writes=[R])


def peer_route(fw, B, W, Wd, ntok):
    pe, act, dve, pool, sp = fw.pe, fw.act, fw.dve, fw.pool, fw.sp
    pst, PB, R = B.pst, B.PB, B.R
    hT = B.hT
    with Scope(B) as sc:
        sc.alloc("keysT", [128, 2, 128], BF16)
        sc.alloc("qT", [128, 16, TS], BF16)
        sc.alloc("wq0", [128, 16, 128], BF16); sc.alloc("wq1", [128, 16, 128], BF16)
        sc.alloc("ssb", [128, 16, 128], F32)
        sc.alloc("s2a", [128, 16, 128], F32); sc.alloc("s2b", [128, 8, 256], F32)
        sc.alloc("topv", [128, 16, 16], F32); sc.alloc("topi", [128, 16, 16], U32); sc.alloc("topif", [128, 16, 16], F32)
        sc.alloc("cand", [128, 8, 256], F32)
        sc.alloc("bestv", [128, 8, 16], F32); sc.alloc("bestj", [128, 8, 16], U32)
        sc.alloc("jau", [128, 8, 16], U32); sc.alloc("jaf", [128, 2, 128], F32)
        sc.alloc("oh", [128, 128, 16], F32)
        sc.alloc("rt", [128, 3, 128], F32)
        sc.alloc("gs", [128, 8], F32)
        sc.alloc("idxT", [128, 3, 128], BF16)
        sc.alloc("E1", [128, 32, 128], BF16); sc.alloc("G1", [128, 32, 128], BF16); sc.alloc("E2", [128, 32, 128], BF16)
        sc.alloc("Wsb", [128, 128, 128], BF16)
        wqs = [B.wq0, B.wq1]
        fw.dma(pool, B.keysT[:].rearrange("p a b -> p (a b)"), W["keysT"][:, :], R("keysT"), writes=[R("keysT")])
        for tile in range(ntok // TS):
            tcols = slice(tile * TS, (tile + 1) * TS)
            for hp in range(16):
                wi = B.nxt("wq", 2)
                wt = wqs[wi]; R_w = R(f"wq{wi}")
                fw.dma(pool, wt[:].rearrange("p a b -> p (a b)"), W["wq"][hp, :, :], R_w, writes=[R_w])
                b = B.bank()
                fw.mm([lambda e, kc=kc: e.matmul(pst[b][:, :], wt[:, kc, :], hT[:, kc, tcols], start=(kc == 0), stop=(kc == 15)) for kc in range(16)],
                      reads=[R("hT"), R_w], writes=[PB[b]])
                copy_evac(fw, hp, B.qT[:, hp, :], pst[b][:, :], [PB[b]], [R("qT")])
            for sub in range(4):
                tok0 = tile * TS + sub * 128
                scols = slice(sub * 128, (sub + 1) * 128)
                for q4 in range(4):
                    b = B.bank()
                    for i in range(4):
                        hp = 4 * q4 + i
                        fw.mm([lambda e: e.matmul(pst[b][:, i * 128:(i + 1) * 128], B.qT[:, hp, scols], B.keysT[:, hp % 2, :], start=True, stop=True)],
                              reads=[R("qT"), R("keysT")], writes=[PB[b]])
                    fw.op(act, lambda e: e.copy(out=B.ssb[:, 4 * q4:4 * q4 + 4, :], in_=pst[b][:, :].rearrange("p (a n) -> p a n", n=128)),
                          reads=[PB[b]], writes=[R("ssb")])
                RV = [R(f"tv{i}") for i in range(16)]; RI = [R(f"ti{i}") for i in range(16)]; RS = [R(f"s2_{i}") for i in range(16)]
                for hp in range(16):
                    fw.op(dve, lambda e: e.max(out=B.topv[:, hp, 0:8], in_=B.ssb[:, hp, :]), reads=[R("ssb")], writes=[RV[hp]])
                for hp in range(16):
                    fw.op(dve, lambda e: e.max_index(out=B.topi[:, hp, 0:8], in_max=B.topv[:, hp, 0:8], in_values=B.ssb[:, hp, :]),
                          reads=[R("ssb"), RV[hp]], writes=[RI[hp]])
                for hp in range(16):
                    fw.op(dve, lambda e: e.match_replace(out=B.s2a[:, hp, :], in_to_replace=B.topv[:, hp, 0:8], in_values=B.ssb[:, hp, :], imm_value=-1e30),
                          reads=[R("ssb"), RV[hp]], writes=[RS[hp]])
                for hp in range(16):
                    fw.op(dve, lambda e: e.max(out=B.topv[:, hp, 8:16], in_=B.s2a[:, hp, :]), reads=[RS[hp]], writes=[RV[hp]])
                for hp in range(16):
                    fw.op(dve, lambda e: e.max_index(out=B.topi[:, hp, 8:16], in_max=B.topv[:, hp, 8:16], in_values=B.s2a[:, hp, :]),
                          reads=[RS[hp], RV[hp]], writes=[RI[hp]])
                fw.op(dve, lambda e: e.tensor_copy(out=B.topif[:], in_=B.topi[:]), reads=RI, writes=[R("topif")])
                tv = B.topv[:].rearrange("p (h t) a -> p h t a", t=2)
                tif = B.topif[:].rearrange("p (h t) a -> p h t a", t=2)
                fw.op(dve, lambda e: e.tensor_tensor(out=B.cand[:].rearrange("p h (a b) -> p h a b", b=16),
                                                     in0=tv[:, :, 0, :].unsqueeze(3).to_broadcast([128, 8, 16, 16]),
                                                     in1=tv[:, :, 1, :].unsqueeze(2).to_broadcast([128, 8, 16, 16]), op=ALU.add),
                      reads=RV, writes=[R("cand")])
                BV = [R(f"bv{i}") for i in range(8)]; BJ = [R(f"bj{i}") for i in range(8)]; BS = [R(f"c2_{i}") for i in range(8)]
                for h in range(8):
                    fw.op(dve, lambda e: e.max(out=B.bestv[:, h, 0:8], in_=B.cand[:, h, :]), reads=[R("cand")], writes=[BV[h]])
                for h in range(8):
                    fw.op(dve, lambda e: e.max_index(out=B.bestj[:, h, 0:8], in_max=B.bestv[:, h, 0:8], in_values=B.cand[:, h, :]),
                          reads=[R("cand"), BV[h]], writes=[BJ[h]])
                for h in range(8):
                    fw.op(dve, lambda e: e.match_replace(out=B.s2b[:, h, :], in_to_replace=B.bestv[:, h, 0:8], in_values=B.cand[:, h, :], imm_value=-1e30),
                          reads=[R("cand"), BV[h]], writes=[BS[h]])
                for h in range(8):
                    fw.op(dve, lambda e: e.max(out=B.bestv[:, h, 8:16], in_=B.s2b[:, h, :]), reads=[BS[h]], writes=[BV[h]])
                for h in range(8):
                    fw.op(dve, lambda e: e.max_index(out=B.bestj[:, h, 8:16], in_max=B.bestv[:, h, 8:16], in_values=B.s2b[:, h, :]),
                          reads=[BS[h], BV[h]], writes=[BJ[h]])
                for p, (opn, val) in enumerate(((ALU.logical_shift_right, 4), (ALU.bitwise_and, 15))):
                    fw.op(dve, lambda e: e.tensor_single_scalar(out=B.jau[:], in_=B.bestj[:], scalar=val, op=opn), reads=BJ, writes=[R("jau")])
                    fw.op(dve, lambda e: e.tensor_copy(out=B.jaf[:, p, :], in_=B.jau[:].rearrange("p h k -> p (h k)")), reads=[R("jau")], writes=[R("jaf")])
                    fw.op(dve, lambda e: e.tensor_tensor(out=B.oh[:], in0=B.iota16[:, :].unsqueeze(1).to_broadcast([128, 128, 16]), in1=B.jaf[:, p, :].unsqueeze(2).to_broadcast([128, 128, 16]), op=ALU.is_equal),
                          reads=[R("jaf"), R("const")], writes=[R("oh")])
                    fw.op(dve, lambda e: e.tensor_tensor(out=B.oh[:].rearrange("p (h k) a -> p h k a", k=16), in0=B.oh[:].rearrange("p (h k) a -> p h k a", k=16),
                                                         in1=tif[:, :, p, :].unsqueeze(2).to_broadcast([128, 8, 16, 16]), op=ALU.mult),
                          reads=[R("oh"), R("topif")], writes=[R("oh")])
                    fw.op(dve, lambda e: e.tensor_reduce(out=B.rt[:, p, :], in_=B.oh[:], axis=AX.X, op=ALU.add), reads=[R("oh")], writes=[R("rt")])
                gv = B.rt[:, 2, :].rearrange("p (h k) -> p h k", k=16)
                fw.op(dve, lambda e: e.tensor_tensor(out=gv, in0=B.bestv[:], in1=B.bestv[:, :, 0:1].to_broadcast([128, 8, 16]), op=ALU.subtract),
                      reads=BV, writes=[R("rt")])
                fw.op(act, lambda e: e.activation(out=B.rt[:, 2, :], in_=B.rt[:, 2, :], func=AF.Exp), reads=[R("rt")], writes=[R("rt")])
                fw.op(dve, lambda e: e.tensor_reduce(out=B.gs[:], in_=gv, axis=AX.X, op=ALU.add), reads=[R("rt")], writes=[R("gs")])
                fw.op(dve, lambda e: e.reciprocal(out=B.gs[:], in_=B.gs[:]), reads=[R("gs")], writes=[R("gs")])
                fw.op(dve, lambda e: e.tensor_tensor(out=gv, in0=gv, in1=B.gs[:].unsqueeze(2).to_broadcast([128, 8, 16]), op=ALU.mult),
                      reads=[R("rt"), R("gs")], writes=[R("rt")])
                b = B.bank()
                fw.mm([lambda e, i=i: e.matmul(pst[b][:, i * 128:(i + 1) * 128], B.rt[:, i, :], B.identf[:, :], start=True, stop=True) for i in range(3)],
                      reads=[R("rt"), R("const")], writes=[PB[b]])
                fw.op(act, lambda e: e.copy(out=B.idxT[:], in_=pst[b][:, 0:384].rearrange("p (a t) -> p a t", t=128)), reads=[PB[b]], writes=[R("idxT")])
                for tc in range(4):
                    t0 = tc * 32
                    io = B.iotab[:, :].unsqueeze(1).to_broadcast([128, 32, 128])
                    fw.op(dve, lambda e: e.tensor_tensor(out=B.E1[:], in0=io, in1=B.idxT[:, 0, t0:t0 + 32].unsqueeze(2).to_broadcast([128, 32, 128]), op=ALU.is_equal),
                          reads=[R("idxT"), R("const")], writes=[R("E1")])
                    fw.op(fw.pool, lambda e: e.tensor_tensor(out=B.G1[:], in0=B.E1[:], in1=B.idxT[:, 2, t0:t0 + 32].unsqueeze(2).to_broadcast([128, 32, 128]), op=ALU.mult),
                          reads=[R("E1"), R("idxT")], writes=[R("G1")])
                    fw.op(fw.pool, lambda e: e.tensor_tensor(out=B.E2[:], in0=io, in1=B.idxT[:, 1, t0:t0 + 32].unsqueeze(2).to_broadcast([128, 32, 128]), op=ALU.subtract),
                          reads=[R("idxT"), R("const")], writes=[R("E2")])
                    e2f = B.E2[:].rearrange("p t i -> p (t i)")
                    fw.op(act, lambda e: e.activation(out=e2f, in_=e2f, func=AF.Square), reads=[R("E2")], writes=[R("E2")])
                    fw.op(act, lambda e: e.activation(out=e2f, in_=e2f, func=AF.Relu, scale=-1.0, bias=1.0), reads=[R("E2")], writes=[R("E2")])
                    for t4 in range(8):
                        b = B.bank()
                        fw.mm([lambda e, i=i: e.matmul(pst[b][:, i * 128:(i + 1) * 128], B.G1[:, 4 * t4 + i, :], B.E2[:, 4 * t4 + i, :], start=True, stop=True)
                               for i in range(4)], reads=[R("G1"), R("E2")], writes=[PB[b]])
                        tl = t0 + 4 * t4
                        fw.op(act, lambda e: e.copy(out=B.Wsb[:, :, tl:tl + 4], in_=pst[b][:, :].rearrange("p (t i) -> p i t", i=128)),
                              reads=[PB[b]], writes=[R("Wsb")])
                for q in range(4):
                    fw.dma(sp, Wd[q * 32:(q + 1) * 32, :, tok0:tok0 + 128].rearrange("j i t -> i j t"), B.Wsb[:, q * 32:(q + 1) * 32, :],
                           R("Wsb"), reads=[R("Wsb")], writes=[R("Wd")])


def peer_dense(fw, B, W, Wd, h_src, x_dst, tokbase, HT):
    pe, act, dve, pool, sp = fw.pe, fw.act, fw.dve, fw.pool, fw.sp
    pst, PB, R = B.pst, B.PB, B.R
    hT = B.hT
    NS = HT // 128
    Vv = W["V"].rearrange("(i j) d -> i j d", j=128)
    GRP = 4
    with Scope(B) as sc:
        sc.alloc("oacc", [128, NS, 2048], F32)
        with Scope(B) as sj:
            for i in range(3):
                sj.alloc(f"ut{i}", [128, 16, 128], BF16)
                sj.alloc(f"wj{i}", [128, HT], BF16)
            for i in range(8):
                sj.alloc(f"vj{i}", [128, 2048], BF16)
            sj.alloc("ga0", [128, TS], BF16); sj.alloc("ga1", [128, TS], BF16)
            for i in range(8):
                sj.alloc(f"cf{i}", [128, HT], BF16)
            uts = [B.ut0, B.ut1, B.ut2]; wjs = [B.wj0, B.wj1, B.wj2]
            vjs = [B.vj0, B.vj1, B.vj2, B.vj3, B.vj4, B.vj5, B.vj6, B.vj7]
            gas = [B.ga0, B.ga1]; cfs = [B.cf0, B.cf1, B.cf2, B.cf3, B.cf4, B.cf5, B.cf6, B.cf7]
            obank = [2]

            def next_obank():
                b = obank[0]
                obank[0] = 2 + (b - 2 + 1) % 6
                return b

            def prep(jg):
                slots = []
                for jj in range(GRP):
                    j = jg * GRP + jj
                    si = B.nxt("pj", 3)
                    vi = (jg % 2) * GRP + jj
                    ci = (jg % 2) * GRP + jj
                    fw.dma(pool, uts[si][:].rearrange("p a b -> p (a b)"), W["UT"][j, :, :], R(f"ut{si}"), writes=[R(f"ut{si}")])
                    fw.dma(pool, vjs[vi][:], Vv[:, j, :], R(f"vj{vi}"), writes=[R(f"vj{vi}")])
                    fw.dma(sp, wjs[si][:], Wd[j, :, :], R(f"wj{si}"), reads=[R("Wd")], writes=[R(f"wj{si}")])
                    for tl in range(HT // TS):
                        b = B.nxt("abank", 2)
                        fw.mm([lambda e, kc=kc: e.matmul(pst[b][:, :], uts[si][:, kc, :], hT[:, kc, tl * TS:(tl + 1) * TS],
                                                         start=(kc == 0), stop=(kc == 15)) for kc in range(16)],
                              reads=[R("hT"), R(f"ut{si}")], writes=[PB[b]])
                        gi = B.nxt("ga", 2)
                        fw.op(act, lambda e: e.activation(out=gas[gi][:], in_=pst[b][:, :], func=AF.Gelu), reads=[PB[b]], writes=[R(f"ga{gi}")])
                        fw.op(dve, lambda e: e.tensor_tensor(out=cfs[ci][:, tl * TS:(tl + 1) * TS], in0=gas[gi][:], in1=wjs[si][:, tl * TS:(tl + 1) * TS], op=ALU.mult),
                              reads=[R(f"ga{gi}"), R(f"wj{si}")], writes=[R(f"cf{ci}")])
                    slots.append((vi, ci))
                return slots

            def outp(jg, slots):
                for s in range(NS):
                    for nb in range(4):
                        bk = next_obank()
                        fw.mm([lambda e, vi=vi, ci=ci, jj=jj: e.matmul(pst[bk][:, :], cfs[ci][:, s * 128:(s + 1) * 128], vjs[vi][:, nb * 512:(nb + 1) * 512],
                                                                    start=(jj == 0), stop=(jj == GRP - 1)) for jj, (vi, ci) in enumerate(slots)],
                              reads=[R(f"cf{ci}") for _, ci in slots] + [R(f"vj{vi}") for vi, _ in slots], writes=[PB[bk]])
                        if jg == 0:
                            fw.op(dve, lambda e: e.tensor_copy(out=B.oacc[:, s, nb * 512:(nb + 1) * 512], in_=pst[bk][:, :]), reads=[PB[bk]], writes=[R(f"oacc{s}")])
                        else:
                            fw.op(dve, lambda e: e.tensor_tensor(out=B.oacc[:, s, nb * 512:(nb + 1) * 512], in0=B.oacc[:, s, nb * 512:(nb + 1) * 512],
                                                                 in1=pst[bk][:, :], op=ALU.add), reads=[PB[bk], R(f"oacc{s}")], writes=[R(f"oacc{s}")])

            NG = 128 // GRP
            cur = prep(0)
            for jg in range(NG):
                nxt = prep(jg + 1) if jg + 1 < NG else None
                outp(jg, cur)
                cur = nxt
        with Scope(B) as s2:
            s2.alloc("hres", [128, 2048], F32)
            s2.alloc("lng", [128, 2048], F32); s2.alloc("lnb", [128, 2048], F32)
            fw.dma(sp, B.lng[:], W["ln2g"][:, :], R("lngb"), writes=[R("lngb")])
            fw.dma(sp, B.lnb[:], W["ln2b"][:, :], R("lngb"), writes=[R("lngb")])
            for s in range(NS):
                tok0 = tokbase + s * 128
                fw.dma(sp, B.hres[:], h_src[tok0:tok0 + 128, :], R("hres"), reads=[R("hdst")], writes=[R("hres")])
                fw.op(dve, lambda e: e.scalar_tensor_tensor(out=B.oacc[:, s, :], in0=B.hres[:], scalar=ALPHA, in1=B.oacc[:, s, :], op0=ALU.mult, op1=ALU.add),
                      reads=[R("hres"), R(f"oacc{s}")], writes=[R(f"oacc{s}")])
                layer_norm(fw, B, B.oacc[:, s, :], R(f"oacc{s}"), B.lng, B.lnb, R("lngb"))
                fw.dma(sp, x_dst[tok0:tok0 + 128, :], B.oacc[:, s, :], R(f"oacc{s}"), reads=[R(f"oacc{s}")], writes=[R("xdst")])


def emit_peer(fw, B, W, T, h_src, x_dst, Wd, ntok, HT=1024):
    pe, act, dve, pool, sp = fw.pe, fw.act, fw.dve, fw.pool, fw.sp
    pst, PB, R = B.pst, B.PB, B.R
    for half in range(ntok // HT):
        tokbase = half * HT
        with Scope(B) as st:
            st.alloc("hT", [128, 16, HT], BF16)
            with Scope(B) as s0:
                s0.alloc("xb", [128, 2048], BF16)
                for s in range(HT // 128):
                    fw.dma(pool, B.xb[:], h_src[tokbase + s * 128:tokbase + (s + 1) * 128, :], R("xb"), reads=[R("hdst")], writes=[R("xb")])
                    for q4 in range(4):
                        b = B.bank()
                        fw.mm([lambda e, dc=dc, i=i: e.matmul(pst[b][:, i * 128:(i + 1) * 128], B.xb[:, dc * 128:(dc + 1) * 128], B.ident[:, :],
                                                              start=True, stop=True) for i, dc in enumerate(range(4 * q4, 4 * q4 + 4))],
                              reads=[R("xb"), R("const")], writes=[PB[b]])
                        copy_evac(fw, q4, B.hT[:, 4 * q4:4 * q4 + 4, s * 128:(s + 1) * 128], pst[b][:, :].rearrange("p (c t) -> p c t", t=128),
                                  [PB[b]], [R("hT")])
            peer_route(fw, B, W, Wd, HT)
            peer_dense(fw, B, W, Wd, h_src, x_dst, tokbase, HT)

import numpy as np, math
NEG = -30000.0
IN_OFF = dict(cu=0, cb=512, cc=1024, sq=1536, sk=2048, sv=2176, cq=2304, ckv=2816, kr=3072, pu=3136, gate=3648)

def tile_lhsT(w):
    K, M = w.shape
    return np.ascontiguousarray(w.reshape(K // 128, 128, M).transpose(1, 0, 2).reshape(128, (K // 128) * M))

def prep_layer(inp, l):
    w_in = inp["w_in"][l]
    o = IN_OFF
    cols = []
    for c in range(4): cols.append(np.arange(o["cu"] + 128 * c, o["cu"] + 128 * (c + 1)))
    for c in range(4): cols.append(np.arange(o["cb"] + 128 * c, o["cb"] + 128 * (c + 1)))
    for c in range(4): cols.append(np.arange(o["cc"] + 128 * c, o["cc"] + 128 * (c + 1)))
    for c in range(4): cols.append(np.arange(o["sq"] + 128 * c, o["sq"] + 128 * (c + 1)))
    for g in range(2):
        k = np.arange(o["sk"] + 64 * g, o["sk"] + 64 * (g + 1)); cols.append(np.concatenate([k, k]))
    cols.append(np.arange(o["sv"], o["sv"] + 128))
    for c in range(4): cols.append(np.arange(o["cq"] + 128 * c, o["cq"] + 128 * (c + 1)))
    for c in range(2): cols.append(np.arange(o["ckv"] + 128 * c, o["ckv"] + 128 * (c + 1)))
    x1 = np.arange(o["kr"], o["kr"] + 32); x2 = np.arange(o["kr"] + 32, o["kr"] + 64)
    cols.append(np.concatenate([x1, x2, x1, x2])); cols.append(np.concatenate([x2, x1, x2, x1]))
    for c in range(4): cols.append(np.arange(o["pu"] + 128 * c, o["pu"] + 128 * (c + 1)))
    assert len(cols) == 31
    wng = np.stack([tile_lhsT(w_in[:, c]) for c in cols])
    wb = inp["w_branch"][l]
    wgb = np.empty((64, 128, 2560), np.float32)
    for i in range(4):
        for dc in range(16):
            wgb[i * 16 + dc, :, :2048] = tile_lhsT(w_in[:, o["gate"] + i * 2048 + dc * 128: o["gate"] + i * 2048 + (dc + 1) * 128])
            wgb[i * 16 + dc, :, 2048:] = tile_lhsT(wb[i][:, dc * 128:(dc + 1) * 128])
    wq = inp["mla_w_q_up"][l]
    qc = [np.arange(192 * h, 192 * h + 128) for h in range(4)]
    def r1(h): return np.arange(192 * h + 128, 192 * h + 160)
    def r2(h): return np.arange(192 * h + 160, 192 * h + 192)
    for pr in range(2):
        h0, h1 = 2 * pr, 2 * pr + 1
        qc.append(np.concatenate([r1(h0), r2(h0), r1(h1), r2(h1)]))
        qc.append(np.concatenate([r2(h0), r1(h0), r2(h1), r1(h1)]))
    wqu = np.stack([tile_lhsT(wq[:, c]) for c in qc])
    wkvu = inp["mla_w_kv_up"][l]
    wkn = np.stack([tile_lhsT(wkvu[:, 256 * h: 256 * h + 128]) for h in range(4)])
    vcols = np.concatenate([np.arange(256 * h + 128, 256 * h + 256) for h in range(4)])
    wkv = tile_lhsT(wkvu[:, vcols])
    wo = inp["w_out"][l]
    wout = np.stack([tile_lhsT(wo[:, nb * 512:(nb + 1) * 512]) for nb in range(4)])
    bc = lambda v: np.ascontiguousarray(np.broadcast_to(v[None, :], (128, v.shape[0]))).astype(np.float32)
    d = dict(
        wng=wng, wgb=wgb, wqu=wqu, wkn=wkn, wkv=wkv, wout=wout,
        poolw=np.ascontiguousarray(inp["pool_w"][l]),
        convw=np.ascontiguousarray(inp["conv_w"][l].reshape(3, 4, 128).transpose(2, 1, 0).reshape(128, 12)),
        qn=np.ascontiguousarray(inp["mla_q_norm"][l].reshape(4, 128).T),
        kvn=np.ascontiguousarray(inp["mla_kv_norm"][l].reshape(2, 128).T),
        pscale=np.ascontiguousarray(inp["pool_scale"][l].reshape(4, 128).T),
        sinks=bc(inp["swa_sinks"][l]),
        ln1g=bc(inp["ln1_g"][l]), ln1b=bc(inp["ln1_b"][l]),
    )
    return d

def const_tables(half, nprev_tiles=4, nown_tiles=4):
    T = {}
    T["ident"] = np.eye(128, dtype=np.float32)
    T["ones"] = np.ones((128, 128), np.float32)
    k = np.arange(128)[:, None]; q = np.arange(512)[None, :]
    T["mlam"] = np.concatenate([np.where(q >= 128 * j + k, 0.0, NEG) for j in range(4)], axis=1).astype(np.float32)
    slopes = (2.0 ** (-8.0 * np.arange(1, 9) / 8)).astype(np.float32)
    s = np.arange(128)[:, None, None]; qq = np.arange(128)[None, None, :]; sl = slopes[None, :, None]
    b0 = np.where(qq < s, -sl * (qq - s + 128), NEG)
    b1 = np.where(qq >= s, -sl * (qq - s), NEG)
    T["swab"] = np.stack([b0, b1], axis=1).reshape(128, -1).astype(np.float32)
    T["swab0"] = (b0 if half == 1 else np.full_like(b0, NEG)).reshape(128, -1).astype(np.float32)
    T["pbias"] = np.full((128, 1), 0.0 if half == 1 else NEG, np.float32)
    pc = np.ones((128, 4, 16), np.float32)
    if half == 0:
        for g, w in enumerate((2, 4, 8, 16)):
            t = np.arange(16)
            pc[:, g, :] = (w / np.minimum(t + 1, w))[None, :]
    T["pcorr"] = pc.reshape(128, 64)
    ntile = nprev_tiles + nown_tiles
    pos = np.zeros(ntile * 512, np.float32)
    own0 = half * 2048 if nprev_tiles else 0
    if nprev_tiles and half == 1:
        pos[:nprev_tiles * 512] = np.arange(nprev_tiles * 512)
    pos[nprev_tiles * 512:] = own0 + np.arange(nown_tiles * 512)
    inv = (10000.0 ** (-np.arange(0, 64, 2, dtype=np.float32) / 64)).astype(np.float32)
    ang = pos[None, :] * inv[:, None]
    cos = np.cos(ang).astype(np.float32); sin = np.sin(ang).astype(np.float32)
    T["cs4"] = np.concatenate([cos, cos, cos, cos], 0)
    T["sn4"] = np.concatenate([-sin, sin, -sin, sin], 0)
    return T

def prep_peer(inp, l):
    wq = inp["peer_w_query"][l]
    d = {}
    d["wq"] = np.stack([tile_lhsT(wq[:, hp * 128:(hp + 1) * 128]) for hp in range(16)])
    sk = inp["peer_sub_keys"][l]
    d["keysT"] = np.ascontiguousarray(sk.transpose(2, 0, 1).reshape(128, 256))
    u = inp["peer_u"][l]
    d["UT"] = np.ascontiguousarray(u.reshape(128, 128, 16, 128).transpose(1, 3, 2, 0).reshape(128, 128, 2048))
    d["V"] = inp["peer_v"][l]
    bc = lambda v: np.ascontiguousarray(np.broadcast_to(v[None, :], (128, v.shape[0]))).astype(np.float32)
    d["ln2g"] = bc(inp["ln2_g"][l]); d["ln2b"] = bc(inp["ln2_b"][l])
    return d

def peer_tables():
    T = {}
    T["iota"] = np.ascontiguousarray(np.broadcast_to(np.arange(128, dtype=np.float32)[None, :], (128, 128)))
    T["iota16"] = np.ascontiguousarray(np.broadcast_to(np.arange(16, dtype=np.float32)[None, :], (128, 16)))
    return T

from concourse.bass_utils import run_bass_kernel_spmd

_CACHE = {}
W_SHAPES = dict(wng=[31, 128, 2048], wgb=[64, 128, 2560], wqu=[8, 128, 512], wkn=[4, 128, 256], wkv=[128, 1024], wout=[4, 128, 8192],
                poolw=[4, 128, 128], convw=[128, 12], qn=[128, 4], kvn=[128, 2], pscale=[128, 4], sinks=[128, 8],
                ln1g=[128, 2048], ln1b=[128, 2048],
                wq=[16, 128, 2048], keysT=[128, 256], UT=[128, 128, 2048], V=[16384, 2048], ln2g=[128, 2048], ln2b=[128, 2048])


def t_shapes(ntiles):
    return dict(ident=[128, 128], ones=[128, 128], mlam=[128, 2048], swab=[128, 2048], swab0=[128, 1024], pbias=[128, 1], pcorr=[128, 64],
                cs4=[128, ntiles * TS], sn4=[128, ntiles * TS], iota=[128, 128], iota16=[128, 16])


def build_program(nprev_t, nown_t, nlayers):
    nc = bass.Bass("TRN2", target_bir_lowering=False)
    fw = FW(nc)
    ntok = nown_t * TS
    din = lambda n, sh: nc.dram_tensor(n, list(sh), F32, kind="ExternalInput").ap()
    T = {k: din("t_" + k, sh) for k, sh in t_shapes(nprev_t + nown_t).items()}
    x_in = din("x_own", [ntok, 2048])
    x_prev = din("x_prev", [nprev_t * TS, 2048]) if nprev_t else None
    x_out = nc.dram_tensor("x_next", [ntok, 2048], F32, kind="ExternalOutput").ap()
    hs = nc.dram_tensor("h_scr", [ntok, 2048], F32, kind="Internal").ap()
    Wd = nc.dram_tensor("w_scr", [128, 128, 1024], BF16, kind="Internal").ap()
    xs = [nc.dram_tensor(f"x_scr{i}", [ntok, 2048], F32, kind="Internal").ap() for i in range(min(2, nlayers - 1))]
    B = LayerBufs(fw)
    load_consts(fw, B, T)
    peer_consts(fw, B, T)
    src = x_in
    for l in range(nlayers):
        W = {k: din(f"w{l}_" + k, sh) for k, sh in W_SHAPES.items()}
        dst = x_out if l == nlayers - 1 else xs[l % 2]
        emit_mixer(fw, B, W, T, src, x_prev, hs, nprev_t, nown_t)
        emit_peer(fw, B, W, T, hs, dst, Wd, ntok)
        src = dst
    fw.barrier()
    return nc


NCORES = 4


def kernel(**inputs):
    inp = {k: np.asarray(v) for k, v in inputs.items()}
    x = np.ascontiguousarray(inp["x"], dtype=np.float32)
    Bt, S, D = x.shape
    depth = inp["w_in"].shape[0]
    nown_t = S // TS
    key = (nown_t, depth)
    if key not in _CACHE:
        _CACHE[key] = build_program(0, nown_t, depth)
    nc = _CACHE[key]
    shared = {}
    tabs = const_tables(0, 0, nown_t)
    tabs.update(peer_tables())
    shared.update({"t_" + k: v for k, v in tabs.items()})
    for l in range(depth):
        Wl = prep_layer(inp, l)
        Wl.update(prep_peer(inp, l))
        shared.update({f"w{l}_" + k: np.ascontiguousarray(v, dtype=np.float32) for k, v in Wl.items()})
    in_maps = []
    for core in range(Bt):
        m = dict(shared)
        m["x_own"] = np.ascontiguousarray(x[core])
        in_maps.append(m)
    res = run_bass_kernel_spmd(nc, in_maps, core_ids=list(range(Bt)))
    out = np.stack([np.asarray(res.results[c]["x_next"], dtype=np.float32) for c in range(Bt)], 0)
    return out
```

```python
import numpy as np

import contextlib
import concourse.bass as bass
import concourse.mybir as mybir

F32 = mybir.dt.float32
BF16 = mybir.dt.bfloat16
U32 = mybir.dt.uint32
AF = mybir.ActivationFunctionType
ALU = mybir.AluOpType
AX = mybir.AxisListType


class Reg:
    __slots__ = ("name", "w", "r", "dsem", "dcnt")

    def __init__(self, name):
        self.name = name
        self.w = {}
        self.r = {}
        self.dsem = None
        self.dcnt = 0


def _merge(d, tok):
    if tok is None:
        return
    sem, val = tok
    k = id(sem)
    if k not in d or d[k][1] < val:
        d[k] = (sem, val)


class Eng:
    def __init__(self, fw, e, name, skip_self=False):
        self.fw = fw
        self.e = e
        self.name = name
        self.sem = fw.new_sem("s_" + name)
        self.cnt = 0
        self.waited = {}
        self.skip_self = skip_self

    def wait(self, tok):
        if tok is None:
            return
        sem, val = tok
        if self.skip_self and sem is self.sem:
            return
        k = id(sem)
        if self.waited.get(k, 0) >= val:
            return
        self.e.wait_ge(sem, val)
        self.waited[k] = val

    def wait_all(self, toks):
        for t in toks:
            self.wait(t)


class FW:
    def __init__(self, nc):
        self.nc = nc
        self.stack = contextlib.ExitStack()
        self.nsem = 0
        self.pe = Eng(self, nc.tensor, "pe", skip_self=True)
        self.act = Eng(self, nc.scalar, "act")
        self.dve = Eng(self, nc.vector, "dve")
        self.pool = Eng(self, nc.gpsimd, "pool")
        self.sp = Eng(self, nc.sync, "sp")
        self.engs = [self.pe, self.act, self.dve, self.pool, self.sp]
        self.all_dma_toks = {}

    def new_sem(self, name):
        self.nsem += 1
        return self.stack.enter_context(self.nc.semaphore(name))

    def sbuf(self, name, shape, dtype, stack=None):
        st = stack or self.stack
        return st.enter_context(self.nc.sbuf_tensor(name, list(shape), dtype))

    def psum(self, name, shape, dtype):
        return self.stack.enter_context(self.nc.psum_tensor(name, list(shape), dtype))

    def _deps(self, reads, writes):
        deps = {}
        for r in reads:
            for t in r.w.values():
                _merge(deps, t)
        for w in writes:
            for t in w.w.values():
                _merge(deps, t)
            for t in w.r.values():
                _merge(deps, t)
        return list(deps.values())

    def _commit(self, tok, reads, writes):
        for r in reads:
            _merge(r.r, tok)
        for w in writes:
            w.w = {}
            w.r = {}
            _merge(w.w, tok)

    def op(self, eng, fn, reads=(), writes=(), extra=()):
        eng.wait_all(self._deps(reads, writes))
        eng.wait_all(extra)
        inst = fn(eng.e)
        inst.then_inc(eng.sem, 1)
        eng.cnt += 1
        tok = (eng.sem, eng.cnt)
        self._commit(tok, reads, writes)
        return tok

    def mm(self, fns, reads=(), writes=()):
        eng = self.pe
        eng.wait_all(self._deps(reads, writes))
        inst = None
        for fn in fns:
            inst = fn(eng.e)
        inst.then_inc(eng.sem, 1)
        eng.cnt += 1
        tok = (eng.sem, eng.cnt)
        self._commit(tok, reads, writes)
        return tok

    def dma(self, eng, out, in_, semreg, reads=(), writes=(), **kw):
        eng.wait_all(self._deps(reads, writes))
        if semreg.dsem is None:
            semreg.dsem = {}
            semreg.dcnt = {}
        if eng.name not in semreg.dsem:
            semreg.dsem[eng.name] = self.new_sem("d_" + semreg.name + "_" + eng.name)
            semreg.dcnt[eng.name] = 0
        sem = semreg.dsem[eng.name]
        eng.e.dma_start(out=out, in_=in_, **kw).then_inc(sem, 16)
        semreg.dcnt[eng.name] += 16
        tok = (sem, semreg.dcnt[eng.name])
        self._commit(tok, reads, writes)
        _merge(self.all_dma_toks, tok)
        return tok

    def barrier(self, engs=None):
        toks = [(e.sem, e.cnt) for e in self.engs if e.cnt > 0]
        toks += list(self.all_dma_toks.values())
        for e in (engs or self.engs):
            e.wait_all(toks)


import math
import contextlib
import numpy as np

D_MODEL = 2048
TS = 512
NEG = -30000.0
ALPHA = 8 ** 0.25
MLA_SCALE = 192 ** -0.5

C_U, C_B, C_C, C_SQ, C_SK, C_SV, C_CQ, C_CKV, C_KA, C_KB, C_PU = 0, 4, 8, 12, 16, 18, 19, 23, 25, 26, 27
N_NG = 31


class Scope:
    uid = 0

    def __init__(self, B):
        self.B = B
        self.stack = contextlib.ExitStack()
        self.names = []

    def __enter__(self):
        return self

    def alloc(self, name, shape, dtype):
        Scope.uid += 1
        t = self.stack.enter_context(self.B.fw.nc.sbuf_tensor(f"{name}_{Scope.uid}", list(shape), dtype))
        setattr(self.B, name, t)
        self.names.append(name)
        return t

    def __exit__(self, *a):
        self.B.fw.barrier()
        self.stack.close()
        for n in self.names:
            setattr(self.B, n, None)
        return False


class LayerBufs:
    def __init__(self, fw):
        self.fw = fw
        self.regs = {}
        self.pst = [fw.psum(f"pb{i}", [128, 512], F32) for i in range(8)]
        self.PB = [Reg(f"pb{i}") for i in range(8)]
        self.rot = 0
        self.rots = {}
        sc = Scope(self)
        self.const_scope = sc
        sc.alloc("ident", [128, 128], BF16)
        sc.alloc("ones", [128, 128], BF16)
        sc.alloc("mlam", [128, 4, 512], BF16)
        sc.alloc("swab", [128, 2, 8, 128], F32)
        sc.alloc("swab0", [128, 8, 128], F32)
        sc.alloc("pbias", [128, 1], F32)
        sc.alloc("pcorr", [128, 4, 16], F32)
        sc.alloc("den", [128, 8], F32)
        sc.alloc("bst", [128, 4, 6], F32)
        sc.alloc("mv", [128, 2], F32)

    def R(self, name):
        if name not in self.regs:
            self.regs[name] = Reg(name)
        return self.regs[name]

    def bank(self):
        b = self.rot
        self.rot = (self.rot + 1) % 6
        return b

    def nxt(self, key, n):
        v = self.rots.get(key, 0)
        self.rots[key] = (v + 1) % n
        return v


def copy_evac(fw, which, out, in_, reads, writes):
    if which % 2 == 0:
        return fw.op(fw.act, lambda e: e.copy(out=out, in_=in_), reads=reads, writes=writes)
    return fw.op(fw.dve, lambda e: e.tensor_copy(out=out, in_=in_), reads=reads, writes=writes)


def load_consts(fw, B, T):
    q = fw.pool
    R = B.R("const")
    fw.dma(q, B.ident[:], T["ident"][:, :], R, writes=[R])
    fw.dma(q, B.ones[:], T["ones"][:, :], R, writes=[R])
    fw.dma(q, B.mlam[:].rearrange("p a b -> p (a b)"), T["mlam"][:, :], R, writes=[R])
    fw.dma(fw.sp, B.swab[:].rearrange("p a b c -> p (a b c)"), T["swab"][:, :], R, writes=[R])
    fw.dma(fw.sp, B.swab0[:].rearrange("p a b -> p (a b)"), T["swab0"][:, :], R, writes=[R])
    fw.dma(fw.sp, B.pbias[:], T["pbias"][:, :], R, writes=[R])
    fw.dma(fw.sp, B.pcorr[:].rearrange("p a b -> p (a b)"), T["pcorr"][:, :], R, writes=[R])


def alloc_mixer_state(B, sc, NK):
    sc.alloc("convw", [128, 4, 3], F32)
    sc.alloc("qn", [128, 4], F32)
    sc.alloc("kvn", [128, 2], F32)
    sc.alloc("pscale", [128, 4], F32)
    sc.alloc("esink", [128, 8], F32)
    sc.alloc("wqu", [128, 8, 4, 128], BF16)
    sc.alloc("wkn", [128, 4, 2, 128], BF16)
    sc.alloc("wkv", [128, 2, 512], BF16)
    sc.alloc("poolw", [128, 4, 128], BF16)
    sc.alloc("ckvnT", [128, 2, NK], BF16)
    sc.alloc("kpeT", [128, NK], BF16)
    sc.alloc("knh", [128, NK], BF16)
    sc.alloc("vh", [128, NK // 128, 129], BF16)
    sc.alloc("zbuf", [128, 4, 2 + TS], F32)
    sc.alloc("ubuf", [128, 4, 16 + TS], F32)
    sc.alloc("kTsw", [128, 2, 128 + TS], BF16)
    sc.alloc("vsw", [128, 5, 2, 65], BF16)
    sc.alloc("mrgT", [128, 16, TS], BF16)


def load_params(fw, B, W):
    q = fw.pool
    R = B.R("par")
    sp = fw.sp
    fw.dma(sp, B.convw[:].rearrange("p a b -> p (a b)"), W["convw"][:, :], R, writes=[R])
    fw.dma(sp, B.qn[:], W["qn"][:, :], R, writes=[R])
    fw.dma(sp, B.kvn[:], W["kvn"][:, :], R, writes=[R])
    fw.dma(sp, B.pscale[:], W["pscale"][:, :], R, writes=[R])
    fw.dma(sp, B.esink[:], W["sinks"][:, :], R, writes=[R])
    fw.dma(q, B.wqu[:].rearrange("p a b c -> p a (b c)"), W["wqu"].rearrange("a p c -> p a c"), R, writes=[R])
    fw.dma(q, B.wkn[:].rearrange("p a b c -> p a (b c)"), W["wkn"].rearrange("a p c -> p a c"), R, writes=[R])
    fw.dma(q, B.wkv[:].rearrange("p a b -> p (a b)"), W["wkv"][:, :], R, writes=[R])
    fw.dma(q, B.poolw[:], W["poolw"].rearrange("a p c -> p a c"), R, writes=[R])
    fw.op(fw.act, lambda e: e.activation(out=B.esink[:], in_=B.esink[:], func=AF.Exp), reads=[R], writes=[R])
    fw.op(fw.dve, lambda e: e.memset(B.vh[:, :, 128:129], 1.0), writes=[B.R("vh")])
    fw.op(fw.dve, lambda e: e.memset(B.vsw[:].rearrange("p a b c -> p (a b c)"), 0.0), writes=[B.R("vsw")])
    fw.op(fw.dve, lambda e: e.memset(B.vsw[:, :, :, 64:65], 1.0), writes=[B.R("vsw")])
    fw.op(fw.dve, lambda e: e.memset(B.zbuf[:].rearrange("p a b -> p (a b)"), 0.0), writes=[B.R("zbuf")])
    fw.op(fw.dve, lambda e: e.memset(B.ubuf[:].rearrange("p a b -> p (a b)"), 0.0), writes=[B.R("ubuf")])
    fw.op(fw.dve, lambda e: e.memset(B.kTsw[:].rearrange("p a b -> p (a b)"), 0.0), writes=[B.R("kTsw")])


def rmsnorm(fw, B, nch, gam, dst, R_dst, nfeat):
    src = B.cqf
    for c in range(nch):
        fw.op(fw.act, lambda e, c=c: e.activation(out=B.sq[:, c, :], in_=src[:, c, :], func=AF.Square),
              reads=[B.R("cqf")], writes=[B.R("sq")])
    b = B.bank()
    fw.mm([lambda e, c=c: e.matmul(B.pst[b][:, :], B.ones[:, :], B.sq[:, c, :], start=(c == 0), stop=(c == nch - 1)) for c in range(nch)],
          reads=[B.R("sq"), B.R("const")], writes=[B.PB[b]])
    fw.op(fw.act, lambda e: e.activation(out=B.rstd[:], in_=B.pst[b][:, :], func=AF.Sqrt, bias=1e-5, scale=1.0 / nfeat),
          reads=[B.PB[b]], writes=[B.R("rstd")])
    fw.op(fw.dve, lambda e: e.reciprocal(out=B.rstd[:], in_=B.rstd[:]), reads=[B.R("rstd")], writes=[B.R("rstd")])
    for c in range(nch):
        fw.op(fw.dve, lambda e, c=c: e.scalar_tensor_tensor(out=dst(c), in0=src[:, c, :], scalar=gam[:, c:c + 1], in1=B.rstd[:],
                                                           op0=ALU.mult, op1=ALU.mult),
              reads=[B.R("cqf"), B.R("rstd"), B.R("par")], writes=[R_dst])


def phase1a(fw, B, W, T, xsrc, tok0, tt, own, halo, first_own):
    pe, act, dve, pool, sp = fw.pe, fw.act, fw.dve, fw.pool, fw.sp
    pst, PB, R = B.pst, B.PB, B.R
    cols = slice(tt * TS, (tt + 1) * TS)
    with Scope(B) as sc:
        sc.alloc("xb", [128, 2048], BF16)
        sc.alloc("wng0", [128, 16, 128], BF16); sc.alloc("wng1", [128, 16, 128], BF16)
        sc.alloc("cs", [128, TS], F32); sc.alloc("sn", [128, TS], F32)
        sc.alloc("cu", [128, TS], F32); sc.alloc("cbb", [128, TS], F32)
        sc.alloc("cqf", [128, 4, TS], F32); sc.alloc("sq", [128, 4, TS], BF16)
        sc.alloc("rstd", [128, TS], F32)
        sc.alloc("t1", [128, TS], F32); sc.alloc("t2", [128, TS], F32)
        sc.alloc("ptmp", [128, 16 + TS], F32); sc.alloc("ptmp2", [128, 16 + TS], F32)
        sc.alloc("py", [128, TS], BF16)
        wngs = [B.wng0, B.wng1]
        xT = B.xT
        fw.dma(sp, B.cs[:], T["cs4"][:, cols], R("cs"), writes=[R("cs")])
        fw.dma(sp, B.sn[:], T["sn4"][:, cols], R("cs"), writes=[R("cs")])
        for s in range(4):
            fw.dma(pool, B.xb[:], xsrc[tok0 + s * 128:tok0 + (s + 1) * 128, :], R("xb"), reads=[R("xdst")], writes=[R("xb")])
            for q4 in range(4):
                b = B.bank()
                fw.mm([lambda e, dc=dc, b=b, i=i: e.matmul(pst[b][:, i * 128:(i + 1) * 128], B.xb[:, dc * 128:(dc + 1) * 128], B.ident[:, :],
                                                          start=True, stop=True) for i, dc in enumerate(range(4 * q4, 4 * q4 + 4))],
                      reads=[R("xb"), R("const")], writes=[PB[b]])
                copy_evac(fw, q4, xT[:, 4 * q4:4 * q4 + 4, s * 128:(s + 1) * 128], pst[b][:, :].rearrange("p (c t) -> p c t", t=128),
                          [PB[b]], [R("xT")])
        if own:
            fw.op(dve, lambda e: e.tensor_copy(out=B.zbuf[:, :, 0:2], in_=B.zbuf[:, :, TS:TS + 2]), reads=[R("zbuf")], writes=[R("zbuf")])
            fw.op(dve, lambda e: e.tensor_copy(out=B.ubuf[:, :, 0:16], in_=B.ubuf[:, :, TS:TS + 16]), reads=[R("ubuf")], writes=[R("ubuf")])
            fw.op(dve, lambda e: e.tensor_copy(out=B.kTsw[:, :, 0:128], in_=B.kTsw[:, :, TS:TS + 128]), reads=[R("kTsw")], writes=[R("kTsw")])
            fw.op(dve, lambda e: e.tensor_copy(out=B.vsw[:, 0, :, 0:64], in_=B.vsw[:, 4, :, 0:64]), reads=[R("vsw")], writes=[R("vsw")])
        conv_order = []
        for c in range(4):
            conv_order += [C_U + c, C_C + c] + ([C_B + c] if own else [])
        rest_own = [C_SQ, C_SQ + 1, C_SQ + 2, C_SQ + 3]
        common = [C_CKV, C_CKV + 1, C_KA, C_KB]
        if own:
            chunks = common + [C_CQ, C_CQ + 1, C_CQ + 2, C_CQ + 3] + rest_own + [C_SK, C_SK + 1, C_SV] + conv_order + [C_PU + g for g in range(4)]
        elif halo:
            chunks = common + [C_SK, C_SK + 1, C_SV] + conv_order + [C_PU + g for g in range(4)]
        else:
            chunks = common
        for ch in chunks:
            wi = B.nxt("wng", 2)
            wt = wngs[wi]; R_w = R(f"wng{wi}")
            fw.dma(pool, wt[:].rearrange("p a b -> p (a b)"), W["wng"][ch, :, :], R_w, writes=[R_w])
            b = B.bank()
            if ch == C_SV:
                for s in range(4):
                    fw.mm([lambda e, s=s, kc=kc: e.matmul(pst[b][:, s * 128:(s + 1) * 128], xT[:, kc, s * 128:(s + 1) * 128], wt[:, kc, :],
                                                          start=(kc == 0), stop=(kc == 15)) for kc in range(16)],
                          reads=[R("xT"), R_w], writes=[PB[b]])
                for s in range(4):
                    copy_evac(fw, 0, B.vsw[:, 1 + s, :, 0:64], pst[b][:, s * 128:(s + 1) * 128].rearrange("p (g d) -> p g d", d=64),
                              [PB[b]], [R("vsw")])
                continue
            fw.mm([lambda e, kc=kc: e.matmul(pst[b][:, :], wt[:, kc, :], xT[:, kc, :], start=(kc == 0), stop=(kc == 15)) for kc in range(16)],
                  reads=[R("xT"), R_w], writes=[PB[b]])
            ps = pst[b][:, :]
            if C_U <= ch < C_U + 4:
                fw.op(act, lambda e: e.copy(out=B.cu[:], in_=ps), reads=[PB[b]], writes=[R("cu")])
            elif C_C <= ch < C_C + 4:
                c = ch - C_C
                fw.op(dve, lambda e: e.tensor_tensor(out=B.zbuf[:, c, 2:2 + TS], in0=ps, in1=B.cu[:], op=ALU.mult),
                      reads=[PB[b], R("cu")], writes=[R("zbuf")])
            elif C_B <= ch < C_B + 4:
                c = ch - C_B
                fw.op(act, lambda e: e.copy(out=B.cbb[:], in_=ps), reads=[PB[b]], writes=[R("cbb")])
                fw.op(dve, lambda e: e.tensor_scalar(out=B.t1[:], in0=B.zbuf[:, c, 0:TS], scalar1=B.convw[:, c, 0:1], scalar2=None, op0=ALU.mult),
                      reads=[R("zbuf"), R("par")], writes=[R("t1")])
                fw.op(dve, lambda e: e.scalar_tensor_tensor(out=B.t1[:], in0=B.zbuf[:, c, 1:1 + TS], scalar=B.convw[:, c, 1:2], in1=B.t1[:],
                                                           op0=ALU.mult, op1=ALU.add), reads=[R("zbuf"), R("par"), R("t1")], writes=[R("t1")])
                fw.op(dve, lambda e: e.scalar_tensor_tensor(out=B.t1[:], in0=B.zbuf[:, c, 2:2 + TS], scalar=B.convw[:, c, 2:3], in1=B.t1[:],
                                                           op0=ALU.mult, op1=ALU.add), reads=[R("zbuf"), R("par"), R("t1")], writes=[R("t1")])
                fw.op(dve, lambda e: e.tensor_tensor(out=B.brT[:, 0, c, :], in0=B.t1[:], in1=B.cbb[:], op=ALU.mult),
                      reads=[R("t1"), R("cbb")], writes=[R("br0")])
            elif C_SQ <= ch < C_SQ + 4:
                c = ch - C_SQ
                fw.op(act, lambda e: e.copy(out=B.qsw[0:64, 2 * c, :], in_=pst[b][0:64, :]), reads=[PB[b]], writes=[R("qsw")])
                fw.op(act, lambda e: e.copy(out=B.qsw[64:128, 2 * c + 1, :], in_=pst[b][64:128, :]), reads=[PB[b]], writes=[R("qsw")])
            elif C_SK <= ch < C_SK + 2:
                copy_evac(fw, ch, B.kTsw[:, ch - C_SK, 128:128 + TS], ps, [PB[b]], [R("kTsw")])
            elif C_CQ <= ch < C_CQ + 4:
                c = ch - C_CQ
                fw.op(act, lambda e: e.copy(out=B.cqf[:, c, :], in_=ps), reads=[PB[b]], writes=[R("cqf")])
                if c == 3:
                    rmsnorm(fw, B, 4, B.qn, lambda c: B.cqn[:, c, :], R("cqn"), 512)
            elif C_CKV <= ch < C_CKV + 2:
                c = ch - C_CKV
                fw.op(act, lambda e: e.copy(out=B.cqf[:, c, :], in_=ps), reads=[PB[b]], writes=[R("cqf")])
                if c == 1:
                    rmsnorm(fw, B, 2, B.kvn, lambda c: B.ckvnT[:, c, cols], R("ckvnT"), 256)
            elif ch == C_KA:
                fw.op(dve, lambda e: e.tensor_tensor(out=B.t1[:], in0=ps, in1=B.cs[:], op=ALU.mult), reads=[PB[b], R("cs")], writes=[R("t1")])
            elif ch == C_KB:
                fw.op(dve, lambda e: e.tensor_tensor(out=B.t2[:], in0=ps, in1=B.sn[:], op=ALU.mult), reads=[PB[b], R("cs")], writes=[R("t2")])
                fw.op(dve, lambda e: e.tensor_tensor(out=B.kpeT[:, cols], in0=B.t1[:], in1=B.t2[:], op=ALU.add),
                      reads=[R("t1"), R("t2")], writes=[R("kpeT")])
            elif C_PU <= ch < C_PU + 4:
                g = ch - C_PU
                fw.op(act, lambda e: e.copy(out=B.ubuf[:, g, 16:16 + TS], in_=ps), reads=[PB[b]], writes=[R("ubuf")])
                if not own:
                    continue
                L = 16 + TS
                src = B.ubuf[:, g, :]
                bufs = [(B.ptmp, R("ptmp")), (B.ptmp2, R("ptmp2"))]
                cur = None
                for lev in range(g + 1):
                    sh = 1 << lev
                    dstb, R_d = bufs[lev % 2]
                    if cur is None:
                        fw.op(dve, lambda e: e.tensor_tensor(out=dstb[:, sh:L], in0=src[:, sh:L], in1=src[:, 0:L - sh], op=ALU.add),
                              reads=[R("ubuf")], writes=[R_d])
                    else:
                        cb_, R_c = cur
                        fw.op(dve, lambda e: e.tensor_tensor(out=dstb[:, 2 * sh - 1:L], in0=cb_[:, 2 * sh - 1:L], in1=cb_[:, sh - 1:L - sh], op=ALU.add),
                              reads=[R_c], writes=[R_d])
                    cur = (dstb, R_d)
                cb_, R_c = cur
                wd = 2 << g
                if first_own:
                    fw.op(dve, lambda e: e.tensor_tensor(out=cb_[:, 16:32], in0=cb_[:, 16:32], in1=B.pcorr[:, g, :], op=ALU.mult),
                          reads=[R_c, R("const")], writes=[R_c])
                fw.op(dve, lambda e: e.scalar_tensor_tensor(out=B.py[:], in0=cb_[:, 16:L], scalar=1.0 / wd, in1=B.ubuf[:, g, 16:L],
                                                           op0=ALU.mult, op1=ALU.subtract), reads=[R_c, R("ubuf")], writes=[R("py")])
                b2 = B.bank()
                fw.mm([lambda e: e.matmul(pst[b2][:, :], B.poolw[:, g, :], B.py[:], start=True, stop=True)],
                      reads=[R("py"), R("par")], writes=[PB[b2]])
                fw.op(dve, lambda e: e.tensor_scalar(out=B.brT[:, 3, g, :], in0=pst[b2][:, :], scalar1=B.pscale[:, g:g + 1], scalar2=None, op0=ALU.mult),
                      reads=[PB[b2], R("par")], writes=[R("br3")])
        if own:
            for h in range(4):
                b = B.bank()
                fw.mm([lambda e, kc=kc: e.matmul(pst[b][:, :], B.wqu[:, h, kc, :], B.cqn[:, kc, :], start=(kc == 0), stop=(kc == 3)) for kc in range(4)],
                      reads=[R("cqn"), R("par")], writes=[PB[b]])
                copy_evac(fw, h, B.qnT[:, h, :], pst[b][:, :], [PB[b]], [R("qnT")])
            for pr in range(2):
                ba = B.bank()
                fw.mm([lambda e, kc=kc: e.matmul(pst[ba][:, :], B.wqu[:, 4 + 2 * pr, kc, :], B.cqn[:, kc, :], start=(kc == 0), stop=(kc == 3)) for kc in range(4)],
                      reads=[R("cqn"), R("par")], writes=[PB[ba]])
                fw.op(dve, lambda e: e.tensor_tensor(out=B.t1[:], in0=pst[ba][:, :], in1=B.cs[:], op=ALU.mult), reads=[PB[ba], R("cs")], writes=[R("t1")])
                bb = B.bank()
                fw.mm([lambda e, kc=kc: e.matmul(pst[bb][:, :], B.wqu[:, 5 + 2 * pr, kc, :], B.cqn[:, kc, :], start=(kc == 0), stop=(kc == 3)) for kc in range(4)],
                      reads=[R("cqn"), R("par")], writes=[PB[bb]])
                fw.op(dve, lambda e: e.tensor_tensor(out=B.t2[:], in0=pst[bb][:, :], in1=B.sn[:], op=ALU.mult), reads=[PB[bb], R("cs")], writes=[R("t2")])
                fw.op(dve, lambda e: e.tensor_tensor(out=B.qpe[0:64, 2 * pr, :], in0=B.t1[0:64, :], in1=B.t2[0:64, :], op=ALU.add),
                      reads=[R("t1"), R("t2")], writes=[R("qpe")])
                fw.op(dve, lambda e: e.tensor_tensor(out=B.qpe[64:128, 2 * pr + 1, :], in0=B.t1[64:128, :], in1=B.t2[64:128, :], op=ALU.add),
                      reads=[R("t1"), R("t2")], writes=[R("qpe")])


def phase1b(fw, B, T, tt, first_own, nprev_chunks):
    pe, act, dve, pool, sp = fw.pe, fw.act, fw.dve, fw.pool, fw.sp
    pst, PB, R = B.pst, B.PB, B.R
    kc0 = tt * 4
    with Scope(B) as sc:
        sc.alloc("scb0", [128, 512], F32); sc.alloc("scb1", [128, 512], F32)
        sc.alloc("eb0", [128, 512], BF16); sc.alloc("eb1", [128, 512], BF16)
        sc.alloc("osw", [128, 512], BF16)
        sc.alloc("omla", [128, 4, 512], BF16)
        scbs = [B.scb0, B.scb1]; ebs = [B.eb0, B.eb1]
        oa = [6, 7]
        import os
        SKIP = os.environ.get("SKIP", "")
        SP = int(os.environ.get("SWAPART", "9"))
        for j in range(4):
            if "swa" in SKIP:
                break
            for g in range(2):
                for c in range(2):
                    b = B.bank()
                    fw.mm([lambda e, hl=hl: e.matmul(
                        pst[b][:, hl * 128:(hl + 1) * 128],
                        B.kTsw[:, g, (j + c) * 128:(j + c + 1) * 128],
                        B.qsw[:, 4 * g + hl, j * 128:(j + 1) * 128],
                        start=True, stop=True) for hl in range(4)],
                        reads=[R("kTsw"), R("qsw")], writes=[PB[b]])
                    ei = B.nxt("eb", 2)
                    if first_own and j == 0 and c == 0:
                        bias = B.swab0[:, 4 * g:4 * g + 4, :]
                    else:
                        bias = B.swab[:, c, 4 * g:4 * g + 4, :]
                    if SP < 2:
                        fw.op(dve, lambda e: e.tensor_copy(out=scbs[ei][:], in_=pst[b][:, :]), reads=[PB[b]], writes=[R(f"scb{ei}")])
                        continue
                    fw.op(dve, lambda e: e.scalar_tensor_tensor(
                        out=scbs[ei][:].rearrange("p (h q) -> p h q", q=128), in0=pst[b][:, :].rearrange("p (h q) -> p h q", q=128),
                        scalar=0.125, in1=bias, op0=ALU.mult, op1=ALU.add), reads=[PB[b], R("const")], writes=[R(f"scb{ei}")])
                    if SP < 3:
                        continue
                    fw.op(act, lambda e: e.activation(out=ebs[ei][:], in_=scbs[ei][:], func=AF.Exp), reads=[R(f"scb{ei}")], writes=[R(f"eb{ei}")])
                    if SP < 4:
                        continue
                    ob = oa[g]
                    fw.mm([lambda e, hl=hl: e.matmul(
                        pst[ob][:, hl * 65:(hl + 1) * 65], ebs[ei][:, hl * 128:(hl + 1) * 128], B.vsw[:, j + c, g, :],
                        start=(c == 0 and hl == 0), stop=(c == 1), skip_group_check=True) for hl in range(4)],
                        reads=[R(f"eb{ei}"), R("vsw")], writes=[PB[ob]])
            if SP < 5:
                continue
            for g in range(2):
                ob = oa[g]
                ov = pst[ob][:, 0:260].rearrange("p (h d) -> p h d", d=65)
                fw.op(dve, lambda e: e.tensor_tensor(out=B.den[:, 4 * g:4 * g + 4], in0=ov[:, :, 64], in1=B.esink[:, 4 * g:4 * g + 4], op=ALU.add),
                      reads=[PB[ob], R("par")], writes=[R("den")])
                fw.op(dve, lambda e: e.reciprocal(out=B.den[:, 4 * g:4 * g + 4], in_=B.den[:, 4 * g:4 * g + 4]), reads=[R("den")], writes=[R("den")])
                fw.op(dve, lambda e: e.tensor_tensor(
                    out=B.osw[:, 256 * g:256 * g + 256].rearrange("p (h d) -> p h d", d=64), in0=ov[:, :, 0:64],
                    in1=B.den[:, 4 * g:4 * g + 4].unsqueeze(2).to_broadcast([128, 4, 64]), op=ALU.mult),
                    reads=[PB[ob], R("den")], writes=[R("osw")])
            if SP < 6:
                continue
            b = B.bank()
            fw.mm([lambda e, c=c: e.matmul(pst[b][:, c * 128:(c + 1) * 128], B.osw[:, c * 128:(c + 1) * 128], B.ident[:, :], start=True, stop=True)
                   for c in range(4)], reads=[R("osw"), R("const")], writes=[PB[b]])
            copy_evac(fw, j, B.brT[:, 1, :, j * 128:(j + 1) * 128], pst[b][:, :].rearrange("p (c q) -> p c q", q=128), [PB[b]], [R("br1")])

        nkc = kc0 + 4
        for h in range(4):
            if "mla" in SKIP:
                break
            ph = 64 * (h % 2)
            for t in range(tt + 1):
                b = B.bank()
                fw.mm([lambda e, kc=kc: e.matmul(pst[b][:, :], B.wkn[:, h, kc, :], B.ckvnT[:, kc, t * TS:(t + 1) * TS], start=(kc == 0), stop=(kc == 1))
                       for kc in range(2)], reads=[R("ckvnT"), R("par")], writes=[PB[b]])
                copy_evac(fw, t, B.knh[:, t * TS:(t + 1) * TS], pst[b][:, :], [PB[b]], [R("knh")])
            for t in range(tt + 1):
                b = B.bank()
                for s in range(4):
                    kcg = 4 * t + s
                    fw.mm([lambda e, kc=kc: e.matmul(pst[b][:, s * 128:(s + 1) * 128], B.ckvnT[:, kc, kcg * 128:(kcg + 1) * 128],
                                                     B.wkv[:, kc, h * 128:(h + 1) * 128], start=(kc == 0), stop=(kc == 1)) for kc in range(2)],
                          reads=[R("ckvnT"), R("par")], writes=[PB[b]])
                copy_evac(fw, t + 1, B.vh[:, 4 * t:4 * t + 4, 0:128], pst[b][:, :].rearrange("p (s d) -> p s d", d=128), [PB[b]], [R("vh")])
            for kcg in range(nkc):
                b = B.bank()
                kcs = slice(kcg * 128, (kcg + 1) * 128)
                j = kcg - kc0
                fns = [lambda e: e.matmul(pst[b][:, :], B.knh[:, kcs], B.qnT[:, h, :], start=True, stop=False),
                       lambda e: e.matmul(pst[b][:, :], B.kpeT[:, kcs], B.qpe[:, h, :], start=False, stop=(j < 0))]
                if j >= 0:
                    fns.append(lambda e: e.matmul(pst[b][:, :], B.ident[:, :], B.mlam[:, j, :], start=False, stop=True))
                fw.mm(fns, reads=[R("knh"), R("kpeT"), R("qnT"), R("qpe"), R("const")], writes=[PB[b]])
                ei = B.nxt("eb", 2)
                isprev = kcg < nprev_chunks
                fw.op(act, lambda e: e.activation(out=ebs[ei][:], in_=pst[b][:, :], func=AF.Exp, scale=MLA_SCALE,
                                                  bias=(B.pbias[:, 0:1] if isprev else 0.0)),
                      reads=[PB[b], R("const")], writes=[R(f"eb{ei}")])
                for s in range(4):
                    if j > s:
                        continue
                    ob = oa[s // 2]
                    fw.mm([lambda e: e.matmul(pst[ob][:, (s % 2) * 129:(s % 2) * 129 + 129], ebs[ei][:, s * 128:(s + 1) * 128],
                                              B.vh[:, kcg, :], start=(kcg == 0 and s % 2 == 0), stop=(kcg == kc0 + s), skip_group_check=True)],
                          reads=[R(f"eb{ei}"), R("vh")], writes=[PB[ob]])
            for s in range(4):
                ob = oa[s // 2]
                o0 = (s % 2) * 129
                fw.op(dve, lambda e: e.reciprocal(out=B.den[:, s:s + 1], in_=pst[ob][:, o0 + 128:o0 + 129]), reads=[PB[ob]], writes=[R("den")])
                fw.op(dve, lambda e: e.tensor_scalar(out=B.omla[:, s, h * 128:(h + 1) * 128], in0=pst[ob][:, o0:o0 + 128],
                                                     scalar1=B.den[:, s:s + 1], scalar2=None, op0=ALU.mult),
                      reads=[PB[ob], R("den")], writes=[R("omla")])
        for s in range(4):
            b = B.bank()
            fw.mm([lambda e, c=c: e.matmul(pst[b][:, c * 128:(c + 1) * 128], B.omla[:, s, c * 128:(c + 1) * 128], B.ident[:, :], start=True, stop=True)
                   for c in range(4)], reads=[R("omla"), R("const")], writes=[PB[b]])
            copy_evac(fw, s, B.brT[:, 2, :, s * 128:(s + 1) * 128], pst[b][:, :].rearrange("p (c q) -> p c q", q=128), [PB[b]], [R("br2")])


def phase2(fw, B, W):
    pe, act, dve, pool, sp = fw.pe, fw.act, fw.dve, fw.pool, fw.sp
    pst, PB, R = B.pst, B.PB, B.R
    xT = B.xT
    with Scope(B) as sc:
        for i in range(3):
            sc.alloc(f"wgb{i}", [128, 2560], BF16)
        sc.alloc("sig0", [128, TS], F32); sc.alloc("sig1", [128, TS], F32)
        sc.alloc("macc", [128, TS], F32); sc.alloc("mtmp", [128, TS], F32)
        wgbs = [B.wgb0, B.wgb1, B.wgb2]; sigs = [B.sig0, B.sig1]
        for dcn in range(16):
            for i in range(4):
                gi = B.nxt("wgb", 3)
                wt = wgbs[gi]; R_w = R(f"wgb{gi}")
                fw.dma(pool, wt[:], W["wgb"][i * 16 + dcn, :, :], R_w, writes=[R_w])
                bg = B.bank()
                fw.mm([lambda e, kc=kc: e.matmul(pst[bg][:, :], wt[:, kc * 128:(kc + 1) * 128], xT[:, kc, :], start=(kc == 0), stop=(kc == 15))
                       for kc in range(16)], reads=[R("xT"), R_w], writes=[PB[bg]])
                bp = B.bank()
                fw.mm([lambda e, cc=cc: e.matmul(pst[bp][:, :], wt[:, 2048 + cc * 128:2048 + (cc + 1) * 128], B.brT[:, i, cc, :],
                                                 start=(cc == 0), stop=(cc == 3)) for cc in range(4)],
                      reads=[R(f"br{i}"), R_w], writes=[PB[bp]])
                si = B.nxt("sig", 2)
                fw.op(act, lambda e: e.activation(out=sigs[si][:], in_=pst[bg][:, :], func=AF.Sigmoid), reads=[PB[bg]], writes=[R(f"sig{si}")])
                if i == 0:
                    fw.op(dve, lambda e: e.tensor_tensor(out=B.macc[:], in0=pst[bp][:, :], in1=sigs[si][:], op=ALU.mult),
                          reads=[PB[bp], R(f"sig{si}")], writes=[R("macc")])
                else:
                    fw.op(dve, lambda e: e.tensor_tensor(out=B.mtmp[:], in0=pst[bp][:, :], in1=sigs[si][:], op=ALU.mult),
                          reads=[PB[bp], R(f"sig{si}")], writes=[R("mtmp")])
                    if i < 3:
                        fw.op(dve, lambda e: e.tensor_tensor(out=B.macc[:], in0=B.macc[:], in1=B.mtmp[:], op=ALU.add),
                              reads=[R("macc"), R("mtmp")], writes=[R("macc")])
                    else:
                        fw.op(dve, lambda e: e.tensor_tensor(out=B.mrgT[:, dcn, :], in0=B.macc[:], in1=B.mtmp[:], op=ALU.add),
                              reads=[R("macc"), R("mtmp")], writes=[R("mrgT")])


def layer_norm(fw, B, y, R_y, gam, bet, R_gb):
    dve, pool, act = fw.dve, fw.pool, fw.act
    R = B.R
    for c in range(4):
        fw.op(dve, lambda e, c=c: e.bn_stats(out=B.bst[:, c, :], in_=y[:, c * 512:(c + 1) * 512]), reads=[R_y], writes=[R("bst")])
    fw.op(dve, lambda e: e.bn_aggr(out=B.mv[:], in_=B.bst[:]), reads=[R("bst")], writes=[R("mv")])
    fw.op(act, lambda e: e.activation(out=B.mv[:, 1:2], in_=B.mv[:, 1:2], func=AF.Sqrt, bias=1e-5, scale=1.0), reads=[R("mv")], writes=[R("mv")])
    fw.op(dve, lambda e: e.reciprocal(out=B.mv[:, 1:2], in_=B.mv[:, 1:2]), reads=[R("mv")], writes=[R("mv")])
    fw.op(dve, lambda e: e.tensor_scalar(out=y, in0=y, scalar1=B.mv[:, 0:1], scalar2=B.mv[:, 1:2], op0=ALU.subtract, op1=ALU.mult),
          reads=[R_y, R("mv")], writes=[R_y])
    fw.op(dve, lambda e: e.tensor_tensor(out=y, in0=y, in1=gam[:], op=ALU.mult), reads=[R_y, R_gb], writes=[R_y])
    fw.op(dve, lambda e: e.tensor_tensor(out=y, in0=y, in1=bet[:], op=ALU.add), reads=[R_y, R_gb], writes=[R_y])


def phase3(fw, B, W, xsrc, tok0, h_dst, hT_dst=None):
    pe, act, dve, pool, sp = fw.pe, fw.act, fw.dve, fw.pool, fw.sp
    pst, PB, R = B.pst, B.PB, B.R
    with Scope(B) as sc:
        sc.alloc("wo0", [128, 16, 512], BF16); sc.alloc("wo1", [128, 16, 512], BF16)
        sc.alloc("xres", [128, 4, 512], F32)
        sc.alloc("yv", [128, 4, 2048], F32)
        sc.alloc("lng", [128, 2048], F32); sc.alloc("lnb", [128, 2048], F32)
        wos = [B.wo0, B.wo1]
        fw.dma(sp, B.lng[:], W["ln1g"][:, :], R("lngb"), writes=[R("lngb")])
        fw.dma(sp, B.lnb[:], W["ln1b"][:, :], R("lngb"), writes=[R("lngb")])
        for nb in range(4):
            wt = wos[nb % 2]; R_w = R(f"wo{nb % 2}")
            fw.dma(pool, wt[:].rearrange("p a b -> p (a b)"), W["wout"][nb, :, :], R_w, writes=[R_w])
            fw.dma(sp, B.xres[:], xsrc[tok0:tok0 + TS, nb * 512:(nb + 1) * 512].rearrange("(s p) d -> p s d", p=128), R("xres"), reads=[R("xdst")], writes=[R("xres")])
            for s in range(4):
                b = B.bank()
                fw.mm([lambda e, kc=kc: e.matmul(pst[b][:, :], B.mrgT[:, kc, s * 128:(s + 1) * 128], wt[:, kc, :], start=(kc == 0), stop=(kc == 15))
                       for kc in range(16)], reads=[R("mrgT"), R_w], writes=[PB[b]])
                fw.op(dve, lambda e: e.scalar_tensor_tensor(out=B.yv[:, s, nb * 512:(nb + 1) * 512], in0=B.xres[:, s, :],
                                                           scalar=ALPHA, in1=pst[b][:, :], op0=ALU.mult, op1=ALU.add),
                      reads=[PB[b], R("xres")], writes=[R(f"yv{s}")])
        for s in range(4):
            layer_norm(fw, B, B.yv[:, s, :], R(f"yv{s}"), B.lng, B.lnb, R("lngb"))
            fw.dma(sp, h_dst[tok0 + s * 128:tok0 + (s + 1) * 128, :], B.yv[:, s, :], R(f"yv{s}"), reads=[R(f"yv{s}")], writes=[R("hdst")])


def emit_mixer(fw, B, W, T, x_own, x_prev, h_dst, nprev_tiles, nown_tiles, dbg=None):
    NK = (nprev_tiles + nown_tiles) * TS
    with Scope(B) as st:
        alloc_mixer_state(B, st, NK)
        load_params(fw, B, W)
        import os
        STOP = os.environ.get("STOP", "")
        for tt in range(nprev_tiles + nown_tiles):
            if STOP.startswith("tile") and tt >= int(STOP[4:]):
                break
            own = tt >= nprev_tiles
            halo = (tt == nprev_tiles - 1)
            first_own = (tt == nprev_tiles)
            xsrc = x_own if own else x_prev
            tok0 = (tt - nprev_tiles) * TS if own else tt * TS
            with Scope(B) as s12:
                s12.alloc("xT", [128, 16, TS], BF16)
                if own:
                    s12.alloc("brT", [128, 4, 4, TS], BF16)
                    s12.alloc("qsw", [128, 8, TS], BF16)
                    s12.alloc("cqn", [128, 4, TS], BF16)
                    s12.alloc("qnT", [128, 4, TS], BF16)
                    s12.alloc("qpe", [128, 4, TS], BF16)
                    fw.op(fw.dve, lambda e: e.memset(B.qsw[:].rearrange("p a b -> p (a b)"), 0.0), writes=[B.R("qsw")])
                    fw.op(fw.dve, lambda e: e.memset(B.qpe[:].rearrange("p a b -> p (a b)"), 0.0), writes=[B.R("qpe")])
                else:
                    s12.alloc("cqn", [128, 4, TS], BF16)
                phase1a(fw, B, W, T, xsrc, tok0, tt, own, halo, first_own)
                if own and STOP == "p1a":
                    break
                if own:
                    phase1b(fw, B, T, tt, first_own, nprev_tiles * 4)
                    if STOP == "p1b":
                        break
                    if dbg is not None:
                        for i in range(4):
                            fw.dma(fw.pool, dbg["brT"][i].rearrange("(c p) t -> p c t", p=128)[:, :, tok0:tok0 + TS], B.brT[:, i, :, :],
                                   B.R(f"br{i}"), reads=[B.R(f"br{i}")])
                    phase2(fw, B, W)
                    if STOP == "p2":
                        break
            if own:
                phase3(fw, B, W, xsrc, tok0, h_dst)
                if STOP == "p3":
                    break


def peer_consts(fw, B, T):
    sc = B.const_scope
    sc.alloc("identf", [128, 128], F32)
    sc.alloc("iotab", [128, 128], BF16)
    sc.alloc("iota16", [128, 16], F32)
    R = B.R("const")
    fw.dma(fw.sp, B.identf[:], T["ident"][:, :], R, writes=[R])
    fw.dma(fw.pool, B.iotab[:], T["iota"][:, :], R, writes=[R])
    fw.dma(fw.sp, B.iota16[:], T["iota16"][:, :], R, writes=[R])


def peer_route(fw, B, W, Wd, ntok):
    pe, act, dve, pool, sp = fw.pe, fw.act, fw.dve, fw.pool, fw.sp
    pst, PB, R = B.pst, B.PB, B.R
    hT = B.hT
    with Scope(B) as sc:
        sc.alloc("keysT", [128, 2, 128], BF16)
        sc.alloc("qT", [128, 16, TS], BF16)
        sc.alloc("wq0", [128, 16, 128], BF16); sc.alloc("wq1", [128, 16, 128], BF16)
        sc.alloc("ssb", [128, 16, 128], F32)
        sc.alloc("s2a", [128, 16, 128], F32); sc.alloc("s2b", [128, 8, 256], F32)
        sc.alloc("topv", [128, 16, 16], F32); sc.alloc("topi", [128, 16, 16], U32); sc.alloc("topif", [128, 16, 16], F32)
        sc.alloc("cand", [128, 8, 256], F32)
        sc.alloc("bestv", [128, 8, 16], F32); sc.alloc("bestj", [128, 8, 16], U32)
        sc.alloc("jau", [128, 8, 16], U32); sc.alloc("jaf", [128, 2, 128], F32)
        sc.alloc("oh", [128, 128, 16], F32)
        sc.alloc("rt", [128, 3, 128], F32)
        sc.alloc("gs", [128, 8], F32)
        sc.alloc("idxT", [128, 3, 128], BF16)
        for i in range(2):
            sc.alloc(f"E1_{i}", [128, 16, 128], BF16); sc.alloc(f"G1_{i}", [128, 16, 128], BF16); sc.alloc(f"E2_{i}", [128, 16, 128], BF16)
        sc.alloc("Wsb", [128, 128, 128], BF16)
        wqs = [B.wq0, B.wq1]
        fw.dma(pool, B.keysT[:].rearrange("p a b -> p (a b)"), W["keysT"][:, :], R("keysT"), writes=[R("keysT")])
        for tile in range(ntok // TS):
            tcols = slice(tile * TS, (tile + 1) * TS)
            for hp in range(16):
                wi = B.nxt("wq", 2)
                wt = wqs[wi]; R_w = R(f"wq{wi}")
                fw.dma(pool, wt[:].rearrange("p a b -> p (a b)"), W["wq"][hp, :, :], R_w, writes=[R_w])
                b = B.bank()
                fw.mm([lambda e, kc=kc: e.matmul(pst[b][:, :], wt[:, kc, :], hT[:, kc, tcols], start=(kc == 0), stop=(kc == 15)) for kc in range(16)],
                      reads=[R("hT"), R_w], writes=[PB[b]])
                copy_evac(fw, hp, B.qT[:, hp, :], pst[b][:, :], [PB[b]], [R("qT")])
            for sub in range(4):
                tok0 = tile * TS + sub * 128
                scols = slice(sub * 128, (sub + 1) * 128)
                for q4 in range(4):
                    b = B.bank()
                    for i in range(4):
                        hp = 4 * q4 + i
                        fw.mm([lambda e: e.matmul(pst[b][:, i * 128:(i + 1) * 128], B.qT[:, hp, scols], B.keysT[:, hp % 2, :], start=True, stop=True)],
                              reads=[R("qT"), R("keysT")], writes=[PB[b]])
                    fw.op(act, lambda e: e.copy(out=B.ssb[:, 4 * q4:4 * q4 + 4, :], in_=pst[b][:, :].rearrange("p (a n) -> p a n", n=128)),
                          reads=[PB[b]], writes=[R("ssb")])
                RV = [R(f"tv{i}") for i in range(16)]; RI = [R(f"ti{i}") for i in range(16)]; RS = [R(f"s2_{i}") for i in range(16)]
                for hp in range(16):
                    fw.op(dve, lambda e: e.max(out=B.topv[:, hp, 0:8], in_=B.ssb[:, hp, :]), reads=[R("ssb")], writes=[RV[hp]])
                for hp in range(16):
                    fw.op(dve, lambda e: e.max_index(out=B.topi[:, hp, 0:8], in_max=B.topv[:, hp, 0:8], in_values=B.ssb[:, hp, :]),
                          reads=[R("ssb"), RV[hp]], writes=[RI[hp]])
                for hp in range(16):
                    fw.op(dve, lambda e: e.match_replace(out=B.s2a[:, hp, :], in_to_replace=B.topv[:, hp, 0:8], in_values=B.ssb[:, hp, :], imm_value=-1e30),
                          reads=[R("ssb"), RV[hp]], writes=[RS[hp]])
                for hp in range(16):
                    fw.op(dve, lambda e: e.max(out=B.topv[:, hp, 8:16], in_=B.s2a[:, hp, :]), reads=[RS[hp]], writes=[RV[hp]])
                for hp in range(16):
                    fw.op(dve, lambda e: e.max_index(out=B.topi[:, hp, 8:16], in_max=B.topv[:, hp, 8:16], in_values=B.s2a[:, hp, :]),
                          reads=[RS[hp], RV[hp]], writes=[RI[hp]])
                fw.op(dve, lambda e: e.tensor_copy(out=B.topif[:], in_=B.topi[:]), reads=RI, writes=[R("topif")])
                tv = B.topv[:].rearrange("p (h t) a -> p h t a", t=2)
                tif = B.topif[:].rearrange("p (h t) a -> p h t a", t=2)
                fw.op(dve, lambda e: e.tensor_tensor(out=B.cand[:].rearrange("p h (a b) -> p h a b", b=16),
                                                     in0=tv[:, :, 0, :].unsqueeze(3).to_broadcast([128, 8, 16, 16]),
                                                     in1=tv[:, :, 1, :].unsqueeze(2).to_broadcast([128, 8, 16, 16]), op=ALU.add),
                      reads=RV, writes=[R("cand")])
                BV = [R(f"bv{i}") for i in range(8)]; BJ = [R(f"bj{i}") for i in range(8)]; BS = [R(f"c2_{i}") for i in range(8)]
                for h in range(8):
                    fw.op(dve, lambda e: e.max(out=B.bestv[:, h, 0:8], in_=B.cand[:, h, :]), reads=[R("cand")], writes=[BV[h]])
                for h in range(8):
                    fw.op(dve, lambda e: e.max_index(out=B.bestj[:, h, 0:8], in_max=B.bestv[:, h, 0:8], in_values=B.cand[:, h, :]),
                          reads=[R("cand"), BV[h]], writes=[BJ[h]])
                for h in range(8):
                    fw.op(dve, lambda e: e.match_replace(out=B.s2b[:, h, :], in_to_replace=B.bestv[:, h, 0:8], in_values=B.cand[:, h, :], imm_value=-1e30),
                          reads=[R("cand"), BV[h]], writes=[BS[h]])
                for h in range(8):
                    fw.op(dve, lambda e: e.max(out=B.bestv[:, h, 8:16], in_=B.s2b[:, h, :]), reads=[BS[h]], writes=[BV[h]])
                for h in range(8):
                    fw.op(dve, lambda e: e.max_index(out=B.bestj[:, h, 8:16], in_max=B.bestv[:, h, 8:16], in_values=B.s2b[:, h, :]),
                          reads=[BS[h], BV[h]], writes=[BJ[h]])
                for p, (opn, val) in enumerate(((ALU.logical_shift_right, 4), (ALU.bitwise_and, 15))):
                    fw.op(dve, lambda e: e.tensor_single_scalar(out=B.jau[:], in_=B.bestj[:], scalar=val, op=opn), reads=BJ, writes=[R("jau")])
                    fw.op(dve, lambda e: e.tensor_copy(out=B.jaf[:, p, :], in_=B.jau[:].rearrange("p h k -> p (h k)")), reads=[R("jau")], writes=[R("jaf")])
                    fw.op(dve, lambda e: e.tensor_tensor(out=B.oh[:], in0=B.iota16[:, :].unsqueeze(1).to_broadcast([128, 128, 16]), in1=B.jaf[:, p, :].unsqueeze(2).to_broadcast([128, 128, 16]), op=ALU.is_equal),
                          reads=[R("jaf"), R("const")], writes=[R("oh")])
                    fw.op(dve, lambda e: e.tensor_tensor(out=B.oh[:].rearrange("p (h k) a -> p h k a", k=16), in0=B.oh[:].rearrange("p (h k) a -> p h k a", k=16),
                                                         in1=tif[:, :, p, :].unsqueeze(2).to_broadcast([128, 8, 16, 16]), op=ALU.mult),
                          reads=[R("oh"), R("topif")], writes=[R("oh")])
                    fw.op(dve, lambda e: e.tensor_reduce(out=B.rt[:, p, :], in_=B.oh[:], axis=AX.X, op=ALU.add), reads=[R("oh")], writes=[R("rt")])
                gv = B.rt[:, 2, :].rearrange("p (h k) -> p h k", k=16)
                fw.op(dve, lambda e: e.tensor_tensor(out=gv, in0=B.bestv[:], in1=B.bestv[:, :, 0:1].to_broadcast([128, 8, 16]), op=ALU.subtract),
                      reads=BV, writes=[R("rt")])
                fw.op(act, lambda e: e.activation(out=B.rt[:, 2, :], in_=B.rt[:, 2, :], func=AF.Exp), reads=[R("rt")], writes=[R("rt")])
                fw.op(dve, lambda e: e.tensor_reduce(out=B.gs[:], in_=gv, axis=AX.X, op=ALU.add), reads=[R("rt")], writes=[R("gs")])
                fw.op(dve, lambda e: e.reciprocal(out=B.gs[:], in_=B.gs[:]), reads=[R("gs")], writes=[R("gs")])
                fw.op(dve, lambda e: e.tensor_tensor(out=gv, in0=gv, in1=B.gs[:].unsqueeze(2).to_broadcast([128, 8, 16]), op=ALU.mult),
                      reads=[R("rt"), R("gs")], writes=[R("rt")])
                b = B.bank()
                fw.mm([lambda e, i=i: e.matmul(pst[b][:, i * 128:(i + 1) * 128], B.rt[:, i, :], B.identf[:, :], start=True, stop=True) for i in range(3)],
                      reads=[R("rt"), R("const")], writes=[PB[b]])
                fw.op(act, lambda e: e.copy(out=B.idxT[:], in_=pst[b][:, 0:384].rearrange("p (a t) -> p a t", t=128)), reads=[PB[b]], writes=[R("idxT")])
                TC = 16
                for tc in range(128 // TC):
                    t0 = tc * TC
                    k = tc % 2
                    E1 = getattr(B, f"E1_{k}"); G1 = getattr(B, f"G1_{k}"); E2 = getattr(B, f"E2_{k}")
                    io = B.iotab[:, :].unsqueeze(1).to_broadcast([128, TC, 128])
                    fw.op(dve, lambda e: e.tensor_tensor(out=E1[:], in0=io, in1=B.idxT[:, 0, t0:t0 + TC].unsqueeze(2).to_broadcast([128, TC, 128]), op=ALU.is_equal),
                          reads=[R("idxT"), R("const")], writes=[R(f"E1_{k}")])
                    fw.op(dve, lambda e: e.tensor_tensor(out=G1[:], in0=E1[:], in1=B.idxT[:, 2, t0:t0 + TC].unsqueeze(2).to_broadcast([128, TC, 128]), op=ALU.mult),
                          reads=[R(f"E1_{k}"), R("idxT")], writes=[R(f"G1_{k}")])
                    fw.op(dve, lambda e: e.tensor_tensor(out=E2[:], in0=io, in1=B.idxT[:, 1, t0:t0 + TC].unsqueeze(2).to_broadcast([128, TC, 128]), op=ALU.is_equal),
                          reads=[R("idxT"), R("const")], writes=[R(f"E2_{k}")])
                    for t4 in range(TC // 4):
                        b = B.bank()
                        fw.mm([lambda e, i=i: e.matmul(pst[b][:, i * 128:(i + 1) * 128], G1[:, 4 * t4 + i, :], E2[:, 4 * t4 + i, :], start=True, stop=True)
                               for i in range(4)], reads=[R(f"G1_{k}"), R(f"E2_{k}")], writes=[PB[b]])
                        tl = t0 + 4 * t4
                        fw.op(act, lambda e: e.copy(out=B.Wsb[:, :, tl:tl + 4], in_=pst[b][:, :].rearrange("p (t i) -> p i t", i=128)),
                              reads=[PB[b]], writes=[R("Wsb")])
                for q in range(4):
                    fw.dma(sp, Wd[q * 32:(q + 1) * 32, :, tok0:tok0 + 128].rearrange("j i t -> i j t"), B.Wsb[:, q * 32:(q + 1) * 32, :],
                           R("Wsb"), reads=[R("Wsb")], writes=[R("Wd")])


def peer_dense(fw, B, W, Wd, h_src, x_dst, tokbase, HT):
    pe, act, dve, pool, sp = fw.pe, fw.act, fw.dve, fw.pool, fw.sp
    pst, PB, R = B.pst, B.PB, B.R
    hT = B.hT
    NS = HT // 128
    Vv = W["V"].rearrange("(i j) d -> i j d", j=128)
    GRP = 4
    with Scope(B) as sc:
        sc.alloc("oacc", [128, NS, 2048], F32)
        with Scope(B) as sj:
            for i in range(3):
                sj.alloc(f"ut{i}", [128, 16, 128], BF16)
                sj.alloc(f"wj{i}", [128, HT], BF16)
            for i in range(8):
                sj.alloc(f"vj{i}", [128, 2048], BF16)
            sj.alloc("ga0", [128, TS], BF16); sj.alloc("ga1", [128, TS], BF16)
            for i in range(8):
                sj.alloc(f"cf{i}", [128, HT], BF16)
            uts = [B.ut0, B.ut1, B.ut2]; wjs = [B.wj0, B.wj1, B.wj2]
            vjs = [B.vj0, B.vj1, B.vj2, B.vj3, B.vj4, B.vj5, B.vj6, B.vj7]
            gas = [B.ga0, B.ga1]; cfs = [B.cf0, B.cf1, B.cf2, B.cf3, B.cf4, B.cf5, B.cf6, B.cf7]
            obank = [2]

            def next_obank():
                b = obank[0]
                obank[0] = 2 + (b - 2 + 1) % 6
                return b

            def prep(jg):
                slots = []
                for jj in range(GRP):
                    j = jg * GRP + jj
                    si = B.nxt("pj", 3)
                    vi = (jg % 2) * GRP + jj
                    ci = (jg % 2) * GRP + jj
                    fw.dma(pool, uts[si][:].rearrange("p a b -> p (a b)"), W["UT"][j, :, :], R(f"ut{si}"), writes=[R(f"ut{si}")])
                    fw.dma(pool, vjs[vi][:], Vv[:, j, :], R(f"vj{vi}"), writes=[R(f"vj{vi}")])
                    fw.dma(sp, wjs[si][:], Wd[j, :, :], R(f"wj{si}"), reads=[R("Wd")], writes=[R(f"wj{si}")])
                    for tl in range(HT // TS):
                        b = B.nxt("abank", 2)
                        fw.mm([lambda e, kc=kc: e.matmul(pst[b][:, :], uts[si][:, kc, :], hT[:, kc, tl * TS:(tl + 1) * TS],
                                                         start=(kc == 0), stop=(kc == 15)) for kc in range(16)],
                              reads=[R("hT"), R(f"ut{si}")], writes=[PB[b]])
                        gi = B.nxt("ga", 2)
                        fw.op(act, lambda e: e.activation(out=gas[gi][:], in_=pst[b][:, :], func=AF.Gelu), reads=[PB[b]], writes=[R(f"ga{gi}")])
                        fw.op(dve, lambda e: e.tensor_tensor(out=cfs[ci][:, tl * TS:(tl + 1) * TS], in0=gas[gi][:], in1=wjs[si][:, tl * TS:(tl + 1) * TS], op=ALU.mult),
                              reads=[R(f"ga{gi}"), R(f"wj{si}")], writes=[R(f"cf{ci}")])
                    slots.append((vi, ci))
                return slots

            def outp(jg, slots):
                for s in range(NS):
                    for nb in range(4):
                        bk = next_obank()
                        fw.mm([lambda e, vi=vi, ci=ci, jj=jj: e.matmul(pst[bk][:, :], cfs[ci][:, s * 128:(s + 1) * 128], vjs[vi][:, nb * 512:(nb + 1) * 512],
                                                                    start=(jj == 0), stop=(jj == GRP - 1)) for jj, (vi, ci) in enumerate(slots)],
                              reads=[R(f"cf{ci}") for _, ci in slots] + [R(f"vj{vi}") for vi, _ in slots], writes=[PB[bk]])
                        if jg == 0:
                            fw.op(dve, lambda e: e.tensor_copy(out=B.oacc[:, s, nb * 512:(nb + 1) * 512], in_=pst[bk][:, :]), reads=[PB[bk]], writes=[R(f"oacc{s}")])
                        else:
                            fw.op(dve, lambda e: e.tensor_tensor(out=B.oacc[:, s, nb * 512:(nb + 1) * 512], in0=B.oacc[:, s, nb * 512:(nb + 1) * 512],
                                                                 in1=pst[bk][:, :], op=ALU.add), reads=[PB[bk], R(f"oacc{s}")], writes=[R(f"oacc{s}")])

            NG = 128 // GRP
            cur = prep(0)
            for jg in range(NG):
                nxt = prep(jg + 1) if jg + 1 < NG else None
                outp(jg, cur)
                cur = nxt
        with Scope(B) as s2:
            s2.alloc("hres", [128, 2048], F32)
            s2.alloc("lng", [128, 2048], F32); s2.alloc("lnb", [128, 2048], F32)
            fw.dma(sp, B.lng[:], W["ln2g"][:, :], R("lngb"), writes=[R("lngb")])
            fw.dma(sp, B.lnb[:], W["ln2b"][:, :], R("lngb"), writes=[R("lngb")])
            for s in range(NS):
                tok0 = tokbase + s * 128
                fw.dma(sp, B.hres[:], h_src[tok0:tok0 + 128, :], R("hres"), reads=[R("hdst")], writes=[R("hres")])
                fw.op(dve, lambda e: e.scalar_tensor_tensor(out=B.oacc[:, s, :], in0=B.hres[:], scalar=ALPHA, in1=B.oacc[:, s, :], op0=ALU.mult, op1=ALU.add),
                      reads=[R("hres"), R(f"oacc{s}")], writes=[R(f"oacc{s}")])
                layer_norm(fw, B, B.oacc[:, s, :], R(f"oacc{s}"), B.lng, B.lnb, R("lngb"))
                fw.dma(sp, x_dst[tok0:tok0 + 128, :], B.oacc[:, s, :], R(f"oacc{s}"), reads=[R(f"oacc{s}")], writes=[R("xdst")])


def emit_peer(fw, B, W, T, h_src, x_dst, Wd, ntok, HT=1024):
    pe, act, dve, pool, sp = fw.pe, fw.act, fw.dve, fw.pool, fw.sp
    pst, PB, R = B.pst, B.PB, B.R
    for half in range(ntok // HT):
        tokbase = half * HT
        with Scope(B) as st:
            st.alloc("hT", [128, 16, HT], BF16)
            with Scope(B) as s0:
                s0.alloc("xb", [128, 2048], BF16)
                for s in range(HT // 128):
                    fw.dma(pool, B.xb[:], h_src[tokbase + s * 128:tokbase + (s + 1) * 128, :], R("xb"), reads=[R("hdst")], writes=[R("xb")])
                    for q4 in range(4):
                        b = B.bank()
                        fw.mm([lambda e, dc=dc, i=i: e.matmul(pst[b][:, i * 128:(i + 1) * 128], B.xb[:, dc * 128:(dc + 1) * 128], B.ident[:, :],
                                                              start=True, stop=True) for i, dc in enumerate(range(4 * q4, 4 * q4 + 4))],
                              reads=[R("xb"), R("const")], writes=[PB[b]])
                        copy_evac(fw, q4, B.hT[:, 4 * q4:4 * q4 + 4, s * 128:(s + 1) * 128], pst[b][:, :].rearrange("p (c t) -> p c t", t=128),
                                  [PB[b]], [R("hT")])
            peer_route(fw, B, W, Wd, HT)
            peer_dense(fw, B, W, Wd, h_src, x_dst, tokbase, HT)

import numpy as np, math
NEG = -30000.0
IN_OFF = dict(cu=0, cb=512, cc=1024, sq=1536, sk=2048, sv=2176, cq=2304, ckv=2816, kr=3072, pu=3136, gate=3648)

def tile_lhsT(w):
    K, M = w.shape
    return np.ascontiguousarray(w.reshape(K // 128, 128, M).transpose(1, 0, 2).reshape(128, (K // 128) * M))

def prep_layer(inp, l):
    w_in = inp["w_in"][l]
    o = IN_OFF
    cols = []
    for c in range(4): cols.append(np.arange(o["cu"] + 128 * c, o["cu"] + 128 * (c + 1)))
    for c in range(4): cols.append(np.arange(o["cb"] + 128 * c, o["cb"] + 128 * (c + 1)))
    for c in range(4): cols.append(np.arange(o["cc"] + 128 * c, o["cc"] + 128 * (c + 1)))
    for c in range(4): cols.append(np.arange(o["sq"] + 128 * c, o["sq"] + 128 * (c + 1)))
    for g in range(2):
        k = np.arange(o["sk"] + 64 * g, o["sk"] + 64 * (g + 1)); cols.append(np.concatenate([k, k]))
    cols.append(np.arange(o["sv"], o["sv"] + 128))
    for c in range(4): cols.append(np.arange(o["cq"] + 128 * c, o["cq"] + 128 * (c + 1)))
    for c in range(2): cols.append(np.arange(o["ckv"] + 128 * c, o["ckv"] + 128 * (c + 1)))
    x1 = np.arange(o["kr"], o["kr"] + 32); x2 = np.arange(o["kr"] + 32, o["kr"] + 64)
    cols.append(np.concatenate([x1, x2, x1, x2])); cols.append(np.concatenate([x2, x1, x2, x1]))
    for c in range(4): cols.append(np.arange(o["pu"] + 128 * c, o["pu"] + 128 * (c + 1)))
    assert len(cols) == 31
    wng = np.stack([tile_lhsT(w_in[:, c]) for c in cols])
    wb = inp["w_branch"][l]
    wgb = np.empty((64, 128, 2560), np.float32)
    for i in range(4):
        for dc in range(16):
            wgb[i * 16 + dc, :, :2048] = tile_lhsT(w_in[:, o["gate"] + i * 2048 + dc * 128: o["gate"] + i * 2048 + (dc + 1) * 128])
            wgb[i * 16 + dc, :, 2048:] = tile_lhsT(wb[i][:, dc * 128:(dc + 1) * 128])
    wq = inp["mla_w_q_up"][l]
    qc = [np.arange(192 * h, 192 * h + 128) for h in range(4)]
    def r1(h): return np.arange(192 * h + 128, 192 * h + 160)
    def r2(h): return np.arange(192 * h + 160, 192 * h + 192)
    for pr in range(2):
        h0, h1 = 2 * pr, 2 * pr + 1
        qc.append(np.concatenate([r1(h0), r2(h0), r1(h1), r2(h1)]))
        qc.append(np.concatenate([r2(h0), r1(h0), r2(h1), r1(h1)]))
    wqu = np.stack([tile_lhsT(wq[:, c]) for c in qc])
    wkvu = inp["mla_w_kv_up"][l]
    wkn = np.stack([tile_lhsT(wkvu[:, 256 * h: 256 * h + 128]) for h in range(4)])
    vcols = np.concatenate([np.arange(256 * h + 128, 256 * h + 256) for h in range(4)])
    wkv = tile_lhsT(wkvu[:, vcols])
    wo = inp["w_out"][l]
    wout = np.stack([tile_lhsT(wo[:, nb * 512:(nb + 1) * 512]) for nb in range(4)])
    bc = lambda v: np.ascontiguousarray(np.broadcast_to(v[None, :], (128, v.shape[0]))).astype(np.float32)
    d = dict(
        wng=wng, wgb=wgb, wqu=wqu, wkn=wkn, wkv=wkv, wout=wout,
        poolw=np.ascontiguousarray(inp["pool_w"][l]),
        convw=np.ascontiguousarray(inp["conv_w"][l].reshape(3, 4, 128).transpose(2, 1, 0).reshape(128, 12)),
        qn=np.ascontiguousarray(inp["mla_q_norm"][l].reshape(4, 128).T),
        kvn=np.ascontiguousarray(inp["mla_kv_norm"][l].reshape(2, 128).T),
        pscale=np.ascontiguousarray(inp["pool_scale"][l].reshape(4, 128).T),
        sinks=bc(inp["swa_sinks"][l]),
        ln1g=bc(inp["ln1_g"][l]), ln1b=bc(inp["ln1_b"][l]),
    )
    return d

def const_tables(half, nprev_tiles=4, nown_tiles=4):
    T = {}
    T["ident"] = np.eye(128, dtype=np.float32)
    T["ones"] = np.ones((128, 128), np.float32)
    k = np.arange(128)[:, None]; q = np.arange(512)[None, :]
    T["mlam"] = np.concatenate([np.where(q >= 128 * j + k, 0.0, NEG) for j in range(4)], axis=1).astype(np.float32)
    slopes = (2.0 ** (-8.0 * np.arange(1, 9) / 8)).astype(np.float32)
    s = np.arange(128)[:, None, None]; qq = np.arange(128)[None, None, :]; sl = slopes[None, :, None]
    b0 = np.where(qq < s, -sl * (qq - s + 128), NEG)
    b1 = np.where(qq >= s, -sl * (qq - s), NEG)
    T["swab"] = np.stack([b0, b1], axis=1).reshape(128, -1).astype(np.float32)
    T["swab0"] = (b0 if half == 1 else np.full_like(b0, NEG)).reshape(128, -1).astype(np.float32)
    T["pbias"] = np.full((128, 1), 0.0 if half == 1 else NEG, np.float32)
    pc = np.ones((128, 4, 16), np.float32)
    if half == 0:
        for g, w in enumerate((2, 4, 8, 16)):
            t = np.arange(16)
            pc[:, g, :] = (w / np.minimum(t + 1, w))[None, :]
    T["pcorr"] = pc.reshape(128, 64)
    ntile = nprev_tiles + nown_tiles
    pos = np.zeros(ntile * 512, np.float32)
    own0 = half * 2048 if nprev_tiles else 0
    if nprev_tiles and half == 1:
        pos[:nprev_tiles * 512] = np.arange(nprev_tiles * 512)
    pos[nprev_tiles * 512:] = own0 + np.arange(nown_tiles * 512)
    inv = (10000.0 ** (-np.arange(0, 64, 2, dtype=np.float32) / 64)).astype(np.float32)
    ang = pos[None, :] * inv[:, None]
    cos = np.cos(ang).astype(np.float32); sin = np.sin(ang).astype(np.float32)
    T["cs4"] = np.concatenate([cos, cos, cos, cos], 0)
    T["sn4"] = np.concatenate([-sin, sin, -sin, sin], 0)
    return T

def prep_peer(inp, l):
    wq = inp["peer_w_query"][l]
    d = {}
    d["wq"] = np.stack([tile_lhsT(wq[:, hp * 128:(hp + 1) * 128]) for hp in range(16)])
    sk = inp["peer_sub_keys"][l]
    d["keysT"] = np.ascontiguousarray(sk.transpose(2, 0, 1).reshape(128, 256))
    u = inp["peer_u"][l]
    d["UT"] = np.ascontiguousarray(u.reshape(128, 128, 16, 128).transpose(1, 3, 2, 0).reshape(128, 128, 2048))
    d["V"] = inp["peer_v"][l]
    bc = lambda v: np.ascontiguousarray(np.broadcast_to(v[None, :], (128, v.shape[0]))).astype(np.float32)
    d["ln2g"] = bc(inp["ln2_g"][l]); d["ln2b"] = bc(inp["ln2_b"][l])
    return d

def peer_tables():
    T = {}
    T["iota"] = np.ascontiguousarray(np.broadcast_to(np.arange(128, dtype=np.float32)[None, :], (128, 128)))
    T["iota16"] = np.ascontiguousarray(np.broadcast_to(np.arange(16, dtype=np.float32)[None, :], (128, 16)))
    return T

from concourse.bass_utils import run_bass_kernel_spmd

_CACHE = {}
W_SHAPES = dict(wng=[31, 128, 2048], wgb=[64, 128, 2560], wqu=[8, 128, 512], wkn=[4, 128, 256], wkv=[128, 1024], wout=[4, 128, 8192],
                poolw=[4, 128, 128], convw=[128, 12], qn=[128, 4], kvn=[128, 2], pscale=[128, 4], sinks=[128, 8],
                ln1g=[128, 2048], ln1b=[128, 2048],
                wq=[16, 128, 2048], keysT=[128, 256], UT=[128, 128, 2048], V=[16384, 2048], ln2g=[128, 2048], ln2b=[128, 2048])


def t_shapes(ntiles):
    return dict(ident=[128, 128], ones=[128, 128], mlam=[128, 2048], swab=[128, 2048], swab0=[128, 1024], pbias=[128, 1], pcorr=[128, 64],
                cs4=[128, ntiles * TS], sn4=[128, ntiles * TS], iota=[128, 128], iota16=[128, 16])


def build_program(nprev_t, nown_t, nlayers):
    nc = bass.Bass("TRN2", target_bir_lowering=False)
    fw = FW(nc)
    ntok = nown_t * TS
    din = lambda n, sh: nc.dram_tensor(n, list(sh), F32, kind="ExternalInput").ap()
    T = {k: din("t_" + k, sh) for k, sh in t_shapes(nprev_t + nown_t).items()}
    x_in = din("x_own", [ntok, 2048])
    x_prev = din("x_prev", [nprev_t * TS, 2048]) if nprev_t else None
    x_out = nc.dram_tensor("x_next", [ntok, 2048], F32, kind="ExternalOutput").ap()
    hs = nc.dram_tensor("h_scr", [ntok, 2048], F32, kind="Internal").ap()
    Wd = nc.dram_tensor("w_scr", [128, 128, 1024], BF16, kind="Internal").ap()
    xs = [nc.dram_tensor(f"x_scr{i}", [ntok, 2048], F32, kind="Internal").ap() for i in range(min(2, nlayers - 1))]
    B = LayerBufs(fw)
    load_consts(fw, B, T)
    peer_consts(fw, B, T)
    src = x_in
    for l in range(nlayers):
        W = {k: din(f"w{l}_" + k, sh) for k, sh in W_SHAPES.items()}
        dst = x_out if l == nlayers - 1 else xs[l % 2]
        emit_mixer(fw, B, W, T, src, x_prev, hs, nprev_t, nown_t)
        emit_peer(fw, B, W, T, hs, dst, Wd, ntok)
        src = dst
    fw.barrier()
    return nc


NCORES = 4


def kernel(**inputs):
    inp = {k: np.asarray(v) for k, v in inputs.items()}
    x = np.ascontiguousarray(inp["x"], dtype=np.float32)
    Bt, S, D = x.shape
    depth = inp["w_in"].shape[0]
    nown_t = S // TS
    key = (nown_t, depth)
    if key not in _CACHE:
        _CACHE[key] = build_program(0, nown_t, depth)
    nc = _CACHE[key]
    shared = {}
    tabs = const_tables(0, 0, nown_t)
    tabs.update(peer_tables())
    shared.update({"t_" + k: v for k, v in tabs.items()})
    for l in range(depth):
        Wl = prep_layer(inp, l)
        Wl.update(prep_peer(inp, l))
        shared.update({f"w{l}_" + k: np.ascontiguousarray(v, dtype=np.float32) for k, v in Wl.items()})
    in_maps = []
    for core in range(Bt):
        m = dict(shared)
        m["x_own"] = np.ascontiguousarray(x[core])
        in_maps.append(m)
    res = run_bass_kernel_spmd(nc, in_maps, core_ids=list(range(Bt)))
    out = np.stack([np.asarray(res.results[c]["x_next"], dtype=np.float32) for c in range(Bt)], 0)
    return out
```
